# Optimizing a Trainium2 kernel written in Bass

```python
import math
import jax, jax.numpy as jnp
from jax import lax
import numpy as np

D_MODEL = 2048
BATCH = 8
SEQ = 2048
DEPTH = 1
DEC_BATCH = 4
DEC_SEQ = 2048
PAST_LEN = 128

GRID_W = 64
HEAD_DIM = 128
NA_HEADS = 8
DN_HEADS = 8
NA_WIDTH = NA_HEADS * HEAD_DIM
DN_WIDTH = DN_HEADS * HEAD_DIM
MIX_WIDTH = NA_WIDTH + DN_WIDTH
NA_ROWS_MAX = 8
NA_COLS = 16
CONV_K = 5
CHUNK = 64
D_FF = -(-8 * D_MODEL // (3 * 256)) * 256
RMS_EPS = 1e-6
L2_EPS = 1e-6

OFF_NA = 0
OFF_DN = OFF_NA + 3 * NA_WIDTH
OFF_Z = OFF_DN + 3 * DN_WIDTH
OFF_B = OFF_Z + DN_WIDTH
OFF_A = OFF_B + 2 * DN_HEADS
IN_COLS = OFF_A + 2 * DN_HEADS

kernel_name = "hybrid_na_gdn_encoder"


def rmsnorm(x, w):
    x32 = x.astype(jnp.float32)
    y = x32 * lax.rsqrt(jnp.mean(x32 * x32, axis=-1, keepdims=True) + RMS_EPS)
    return (y * w.astype(jnp.float32)).astype(x.dtype)


def l2norm(x):
    return x * lax.rsqrt(jnp.sum(x * x, axis=-1, keepdims=True) + L2_EPS)


def neighbourhood_attention(q, k, v, rpb):
    B, L, H, d = q.shape
    rows = L // GRID_W
    kr = min(NA_ROWS_MAX, rows)
    grid = lambda t: t.reshape(B, rows, GRID_W, H, d)
    qg, kg, vg = grid(q * (d ** -0.5)), grid(k), grid(v)
    cols = jnp.arange(GRID_W)
    col_start = jnp.clip(cols - NA_COLS // 2, 0, GRID_W - NA_COLS)
    col_idx = col_start[:, None] + jnp.arange(NA_COLS)[None, :]
    dc_idx = col_idx - cols[:, None] + (NA_COLS - 1)

    def row_block(r):
        rs = jnp.clip(r - kr // 2, 0, rows - kr)
        q_r = lax.dynamic_index_in_dim(qg, r, axis=1, keepdims=False)
        k_band = lax.dynamic_slice_in_dim(kg, rs, kr, axis=1)
        v_band = lax.dynamic_slice_in_dim(vg, rs, kr, axis=1)
        k_win = k_band[:, :, col_idx]
        v_win = v_band[:, :, col_idx]
        dr_idx = rs + jnp.arange(kr) - r + (NA_ROWS_MAX - 1)
        bias = rpb[:, dr_idx[None, :, None], dc_idx[:, None, :]]
        s = jnp.einsum('bchd,brcwhd->bhcrw', q_r, k_win).astype(jnp.float32) + bias.astype(jnp.float32)
        p = jax.nn.softmax(s.reshape(B, H, GRID_W, kr * NA_COLS), axis=-1)
        p = p.reshape(s.shape).astype(v.dtype)
        return jnp.einsum('bhcrw,brcwhd->bchd', p, v_win)

    out = lax.map(row_block, jnp.arange(rows))
    return out.transpose(1, 0, 2, 3, 4).reshape(B, L, H * d)


def centred_depthwise_conv(x, w):
    C = x.shape[-1]
    return lax.conv_general_dilated(
        x, w[:, None, :].astype(x.dtype), window_strides=(1,),
        padding=[(CONV_K // 2, CONV_K // 2)],
        dimension_numbers=('NWC', 'WIO', 'NWC'), feature_group_count=C)


def chunk_gated_delta_rule(q, k, v, g, beta):
    B, L, H, dk = q.shape
    n = L // CHUNK
    chunks = lambda t: t.reshape(B, n, CHUNK, H, -1).transpose(0, 3, 1, 2, 4)
    q = chunks(q * (dk ** -0.5))
    k, v = chunks(k), chunks(v)
    g = jnp.cumsum(g.reshape(B, n, CHUNK, H).transpose(0, 3, 1, 2), axis=-1)
    beta = beta.reshape(B, n, CHUNK, H).transpose(0, 3, 1, 2)
    tri = jnp.tril(jnp.ones((CHUNK, CHUNK), dtype=bool))
    strict = jnp.tril(jnp.ones((CHUNK, CHUNK), dtype=bool), k=-1)
    diff = g[..., :, None] - g[..., None, :]
    decay = jnp.where(tri, jnp.exp(jnp.where(tri, diff, 0.0)), 0.0)
    k_beta = k * beta[..., None]
    v_beta = v * beta[..., None]
    a_mat = jnp.where(strict, jnp.einsum('bhncd,bhnsd->bhncs', k_beta, k) * decay, 0.0)
    eye = jnp.eye(CHUNK, dtype=jnp.float32)
    t_mat = lax.linalg.triangular_solve(eye + a_mat, jnp.broadcast_to(eye, a_mat.shape),
                                        left_side=True, lower=True, unit_diagonal=True)
    u = jnp.einsum('bhncs,bhnse->bhnce', t_mat, v_beta)
    w = jnp.einsum('bhncs,bhnsd->bhncd', t_mat, k_beta * jnp.exp(g)[..., None])
    qk = jnp.where(tri, jnp.einsum('bhncd,bhnsd->bhncs', q, k) * decay, 0.0)

    def step(S, inp):
        q_i, k_i, u_i, w_i, g_i, qk_i = inp
        v_new = u_i - jnp.einsum('bhcd,bhde->bhce', w_i, S)
        o = jnp.einsum('bhcd,bhde->bhce', q_i * jnp.exp(g_i)[..., None], S) \
            + jnp.einsum('bhcs,bhse->bhce', qk_i, v_new)
        g_last = g_i[..., -1]
        S = S * jnp.exp(g_last)[..., None, None] + jnp.einsum(
            'bhcd,bhce->bhde', k_i * jnp.exp(g_last[..., None] - g_i)[..., None], v_new)
        return S, o

    to_scan = lambda t: jnp.moveaxis(t, 2, 0)
    xs = (to_scan(q), to_scan(k), to_scan(u), to_scan(w), to_scan(g), to_scan(qk))
    S0 = jnp.zeros((B, H, dk, v.shape[-1]), jnp.float32)
    _, o = lax.scan(step, S0, xs)
    return o.transpose(1, 0, 3, 2, 4).reshape(B, L, H, -1)


def gated_deltanet_bidir(qkv, z, b, a, conv_w, A_log, dt_bias, norm_w):
    B, L, _ = qkv.shape
    qkv = jax.nn.silu(centred_depthwise_conv(qkv, conv_w).astype(jnp.float32))
    q, k, v = jnp.split(qkv, 3, axis=-1)
    heads = lambda t: t.reshape(B, L, DN_HEADS, HEAD_DIM)
    q, k, v = l2norm(heads(q)), l2norm(heads(k)), heads(v)
    beta = jax.nn.sigmoid(b.astype(jnp.float32)).reshape(B, L, 2, DN_HEADS)
    g = -jnp.exp(A_log.astype(jnp.float32)) * jax.nn.softplus(
        a.astype(jnp.float32).reshape(B, L, 2, DN_HEADS) + dt_bias.astype(jnp.float32))
    flip = lambda t: jnp.flip(t, axis=1)
    o_fwd = chunk_gated_delta_rule(q, k, v, g[:, :, 0], beta[:, :, 0])
    o_bwd = flip(chunk_gated_delta_rule(flip(q), flip(k), flip(v), flip(g[:, :, 1]), flip(beta[:, :, 1])))
    o = o_fwd + o_bwd
    o = o * lax.rsqrt(jnp.mean(o * o, axis=-1, keepdims=True) + RMS_EPS) * norm_w.astype(jnp.float32)
    o = o * jax.nn.silu(heads(z).astype(jnp.float32))
    return o.reshape(B, L, DN_WIDTH)


def encoder_layer(h, norm_mix_w, w_in, na_rpb, dn_conv_w, dn_A_log, dn_dt_bias, dn_norm_w,
                  w_out, norm_ffn_w, w_gate, w_up, w_down):
    B, L, _ = h.shape
    xn = rmsnorm(h, norm_mix_w)
    proj = xn @ w_in
    na = lambda i: proj[..., OFF_NA + i * NA_WIDTH: OFF_NA + (i + 1) * NA_WIDTH].reshape(
        B, L, NA_HEADS, HEAD_DIM)
    y_na = neighbourhood_attention(na(0), na(1), na(2), na_rpb)
    y_dn = gated_deltanet_bidir(proj[..., OFF_DN:OFF_Z], proj[..., OFF_Z:OFF_B],
                                proj[..., OFF_B:OFF_A], proj[..., OFF_A:IN_COLS],
                                dn_conv_w, dn_A_log, dn_dt_bias, dn_norm_w)
    mixed = jnp.concatenate([y_na, y_dn.astype(h.dtype)], axis=-1) @ w_out
    h = h + mixed
    hn = rmsnorm(h, norm_ffn_w)
    return h + (jax.nn.silu(hn @ w_gate) * (hn @ w_up)) @ w_down


def trunk(x, norm_mix_w, w_in, na_rpb, dn_conv_w, dn_A_log, dn_dt_bias, dn_norm_w,
          w_out, norm_ffn_w, w_gate, w_up, w_down, final_norm_w):
    h = x
    for i in range(DEPTH):
        h = encoder_layer(h, norm_mix_w[i], w_in[i], na_rpb[i], dn_conv_w[i], dn_A_log[i],
                          dn_dt_bias[i], dn_norm_w[i], w_out[i], norm_ffn_w[i],
                          w_gate[i], w_up[i], w_down[i])
    return rmsnorm(h, final_norm_w)


def setup_inputs(seed: int = 0) -> dict:
    key = jax.random.key(seed)
    ks = jax.random.split(key, 16)
    f32 = jnp.float32
    nrm = lambda k, shape, fan_in: jax.random.normal(k, shape, f32) * (fan_in ** -0.5)
    gain = lambda k, shape: 1.0 + 0.02 * jax.random.normal(k, shape, f32)
    dt = jnp.exp(jax.random.uniform(ks[7], (DEPTH, 2, DN_HEADS), f32,
                                    math.log(1e-3), math.log(1e-1)))
    return {
        "x_prompt": jax.random.normal(ks[0], (BATCH, SEQ, D_MODEL), f32),
        "x_sample": jax.random.normal(ks[1], (DEC_BATCH, DEC_SEQ, D_MODEL), f32),
        "norm_mix_w": gain(ks[2], (DEPTH, D_MODEL)),
        "w_in": nrm(ks[3], (DEPTH, D_MODEL, IN_COLS), D_MODEL),
        "na_rpb": 0.02 * jax.random.normal(ks[4], (DEPTH, NA_HEADS, 2 * NA_ROWS_MAX - 1, 2 * NA_COLS - 1), f32),
        "dn_conv_w": nrm(ks[5], (DEPTH, CONV_K, 3 * DN_WIDTH), CONV_K),
        "dn_A_log": jnp.log(jax.random.uniform(ks[6], (DEPTH, 2, DN_HEADS), f32, 1.0, 16.0)),
        "dn_dt_bias": dt + jnp.log(-jnp.expm1(-dt)),
        "dn_norm_w": gain(ks[8], (DEPTH, HEAD_DIM)),
        "w_out": nrm(ks[9], (DEPTH, MIX_WIDTH, D_MODEL), MIX_WIDTH),
        "norm_ffn_w": gain(ks[10], (DEPTH, D_MODEL)),
        "w_gate": nrm(ks[11], (DEPTH, D_MODEL, D_FF), D_MODEL),
        "w_up": nrm(ks[12], (DEPTH, D_MODEL, D_FF), D_MODEL),
        "w_down": nrm(ks[13], (DEPTH, D_FF, D_MODEL), D_FF),
        "final_norm_w": gain(ks[14], (D_MODEL,)),
    }


def reference(x_prompt, x_sample, norm_mix_w, w_in, na_rpb, dn_conv_w, dn_A_log, dn_dt_bias,
              dn_norm_w, w_out, norm_ffn_w, w_gate, w_up, w_down, final_norm_w):
    y_prompt = trunk(x_prompt, norm_mix_w, w_in, na_rpb, dn_conv_w, dn_A_log, dn_dt_bias,
                     dn_norm_w, w_out, norm_ffn_w, w_gate, w_up, w_down, final_norm_w)
    y_sample = trunk(x_sample, norm_mix_w, w_in, na_rpb, dn_conv_w, dn_A_log, dn_dt_bias,
                     dn_norm_w, w_out, norm_ffn_w, w_gate, w_up, w_down, final_norm_w)
    return (y_prompt, y_sample)
```

```python
import contextlib
import numpy as np
import concourse.bass as bass
import concourse.mybir as mybir
from concourse.bass_utils import run_bass_kernel_spmd

F32 = mybir.dt.float32
BF16 = mybir.dt.bfloat16
AF = mybir.ActivationFunctionType
ALU = mybir.AluOpType
AX = mybir.AxisListType

D = 2048
L = 2048
KT = 16
TT = 16
HD = 128
IN_COLS = 7200
OFF_DN = 3072
OFF_Z = 6144
OFF_B = 7168
OFF_A = 7184
DFF = 5632
FC = DFF // 128
NEG = -30000.0
EPS = 1e-6


class Op:
    __slots__ = ("eng", "fn", "deps", "dma", "n", "sem", "semval", "prewait")


class Sched:
    CE = ("pe", "act", "dve", "pool")

    def __init__(self, nc, stack, n_dma_sems=8):
        self.nc = nc
        self.ops = []
        self.last_w = {}
        self.readers = {}
        self.ndma = n_dma_sems
        self.csem = {e: stack.enter_context(nc.semaphore("cs_" + e)) for e in self.CE}
        self.dsem = {
            q: [stack.enter_context(nc.semaphore("ds_%s%d" % (q, i))) for i in range(n_dma_sems)]
            for q in ("sp", "act", "pool")
        }
        self.dcount = {"sp": 0, "act": 0, "pool": 0}
        self.cnt = {e: 0 for e in self.CE}
        self.final_dma = {}
        self.nops = 0

    def add(self, eng, fn, reads=(), writes=(), dma=False):
        op = Op()
        op.eng, op.fn, op.dma = eng, fn, dma
        op.prewait = None
        deps = []
        for t in reads:
            w = self.last_w.get(t)
            if w is not None:
                deps.append(w)
        for t in writes:
            w = self.last_w.get(t)
            if w is not None:
                deps.append(w)
            r = self.readers.get(t)
            if r:
                deps.extend(r.values())
        for t in writes:
            self.last_w[t] = op
            self.readers[t] = {}
        for t in reads:
            d = self.readers.setdefault(t, {})
            if dma:
                d[("dma", id(op))] = op
            else:
                d[eng] = op
        if dma:
            j = self.dcount[eng]
            self.dcount[eng] += 1
            op.sem = self.dsem[eng][j % self.ndma]
            op.semval = 16 * (j // self.ndma + 1)
            if j >= self.ndma:
                op.prewait = (op.sem, 16 * (j // self.ndma))
        else:
            self.cnt[eng] += 1
            op.n = self.cnt[eng]
        op.deps = [d for d in deps if d is not op]
        self.ops.append(op)
        self.nops += 1
        return op

    def emit(self, final=False):
        per = {e: [] for e in ("pe", "act", "dve", "pool", "sp")}
        for op in self.ops:
            per[op.eng].append(op)
            if op.dma:
                self.final_dma[id(op.sem)] = (op.sem, op.semval)
        final_dma = self.final_dma

        def run(engname, eng):
            waited = {}
            for op in per[engname]:
                if op.prewait is not None:
                    s, v = op.prewait
                    if waited.get(id(s), 0) < v:
                        eng.wait_ge(s, v)
                        waited[id(s)] = v
                for d in op.deps:
                    if d.dma:
                        s, v = d.sem, d.semval
                    else:
                        if d.eng == engname and d.eng == "pe" and not op.dma:
                            continue
                        s, v = self.csem[d.eng], d.n
                    if waited.get(id(s), 0) < v:
                        eng.wait_ge(s, v)
                        waited[id(s)] = v
                ins = op.fn(eng)
                if op.dma:
                    ins.then_inc(op.sem, 16)
                else:
                    ins.then_inc(self.csem[op.eng], 1)
            if engname == "sp":
                for s, v in final_dma.values():
                    if waited.get(id(s), 0) < v:
                        eng.wait_ge(s, v)

        with self.nc.Block() as block:
            @block.tensor
            def _(e):
                run("pe", e)

            @block.scalar
            def _(e):
                run("act", e)

            @block.vector
            def _(e):
                run("dve", e)

            @block.gpsimd
            def _(e):
                run("pool", e)

            @block.sync
            def _(e):
                run("sp", e)
        self.ops = []


class Ring:
    def __init__(self, tiles, name):
        self.tiles, self.name, self.i = tiles, name, 0

    def next(self):
        k = self.i % len(self.tiles)
        self.i += 1
        return self.tiles[k], "%s%d" % (self.name, k)


def make_consts():
    i = np.arange(128)
    t = i[:, None]
    j = i[None, :]
    c = np.zeros((128, 11, 128), np.float32)
    c[:, 0] = (t == j)
    c[:, 1] = (t <= j)
    c[:, 2] = (t >= j)
    c[:, 3] = (t > j)
    c[:, 4] = (t < j)
    c[:, 5] = np.where(t >= j, 0.0, NEG)
    c[:, 6] = np.where(t <= j, 0.0, NEG)
    c[:, 7] = (t != j)
    c[:, 8] = 1.0
    c[:, 9] = -1.0
    c[:, 10] = NEG
    return c.reshape(128, 11 * 128)


def build(nslot=2, dbg=False, phases="AGNDOF"):
    nc = bass.Bass("TRN2", target_bir_lowering=False)
    dt_in = lambda name, shape, dt=F32: nc.dram_tensor(name, list(shape), dt, kind="ExternalInput").ap()
    x = dt_in("x", [nslot, L, D])
    w_in = dt_in("w_in", [D, IN_COLS])
    w_out = dt_in("w_out", [D, D])
    w_gate = dt_in("w_gate", [D, DFF])
    w_up = dt_in("w_up", [D, DFF])
    w_down = dt_in("w_down", [DFF, D])
    consts_d = dt_in("consts", [128, 11 * 128])
    nw_mix_d = dt_in("nw_mix", [128, KT])
    nw_ffn_d = dt_in("nw_ffn", [128, KT])
    nw_fin_d = dt_in("nw_fin", [1, D])
    rpb_d = dt_in("rpb", [1, 8 * 15 * 31])
    convw_d = dt_in("convw", [128, 24 * 5])
    alog_d = dt_in("alog", [1, 16])
    dtb_d = dt_in("dtb", [1, 16])
    dnw_d = dt_in("dnw", [1, 128])
    y = nc.dram_tensor("y", [nslot, L, D], F32, kind="ExternalOutput").ap()
    mixk = "ExternalOutput" if dbg else "Internal"
    mix_d = nc.dram_tensor("mix_d", [nslot, D, L], BF16, kind=mixk).ap()
    h_d = nc.dram_tensor("h_d", [nslot, L, D], F32, kind=mixk).ap()
    wg_s = nc.dram_tensor("wg_s", [FC // 2, 128, KT, 256], BF16, kind="Internal").ap()
    wu_s = nc.dram_tensor("wu_s", [FC // 2, 128, KT, 256], BF16, kind="Internal").ap()
    wd_s = nc.dram_tensor("wd_s", [FC // 4, 4, 128, 4, 512], BF16, kind="Internal").ap()
    xn_s = nc.dram_tensor("xn_s", [KT, 128, L], BF16, kind="Internal").ap()
    tb_h = nc.dram_tensor("tb_d", [64, 8, 15, 64], F32, kind="Internal")
    tb_d = tb_h.ap()

    with contextlib.ExitStack() as st:
        S = Sched(nc, st)
        uid = [0]

        def sbt(stack, name, shape, dt):
            uid[0] += 1
            return stack.enter_context(nc.sbuf_tensor("%s_u%d" % (name, uid[0]), list(shape), dt))

        def pst(stack, name, shape, dt):
            uid[0] += 1
            return stack.enter_context(nc.psum_tensor("%s_u%d" % (name, uid[0]), list(shape), dt))
        PE = lambda fn, r, w: S.add("pe", fn, r, w)
        ACT = lambda fn, r, w: S.add("act", fn, r, w)
        DVE = lambda fn, r, w: S.add("dve", fn, r, w)
        POOL = lambda fn, r, w: S.add("pool", fn, r, w)
        DSP = lambda fn, r, w: S.add("sp", fn, r, w, dma=True)
        DPL = lambda fn, r, w: S.add("pool", fn, r, w, dma=True)

        R1 = None
        cst = sbt(st, "cst", [128, 11, 128], F32)
        ident_bf = sbt(st, "ident_bf", [128, 128], BF16)
        ones_bf = sbt(st, "ones_bf", [128, 128], BF16)
        nw_mix = sbt(st, "nw_mix", [128, KT], F32)
        nw_ffn = sbt(st, "nw_ffn", [128, KT], F32)
        convw = sbt(st, "convw", [128, 24, 5], F32)
        alog = sbt(st, "alog", [128, 16], F32)
        negA = sbt(st, "negA", [128, 16], F32)
        dtb = sbt(st, "dtb", [128, 16], F32)
        dnw = sbt(st, "dnw", [128, 128], F32)
        beta_all = sbt(st, "beta_all", [128, TT, 16], F32)
        g_all = sbt(st, "g_all", [128, TT, 16], F32)
        ssp = sbt(st, "ssp", [128, TT, 4], F32)
        ss_all = sbt(st, "ss_all", [128, TT], F32)
        rstd_all = sbt(st, "rstd_all", [128, TT], F32)
        epsc = sbt(st, "epsc", [128, 1], F32)
        ident_f = cst[:, 0, :]
        ones_f = cst[:, 8, :]
        negones_f = cst[:, 9, :]

        with contextlib.ExitStack() as s0:
            rpb_sb = sbt(s0, "rpb_sb", [64, 8, 15, 31], F32)
            negt = sbt(s0, "negt", [64, 960], F32)
            DSP(lambda e: e.dma_start(out=cst[:].rearrange("p a b -> p (a b)"), in_=consts_d), [], ["cst"])
            DSP(lambda e: e.dma_start(out=nw_mix[:], in_=nw_mix_d), [], ["nw_mix"])
            DSP(lambda e: e.dma_start(out=nw_ffn[:], in_=nw_ffn_d), [], ["nw_ffn"])
            DSP(lambda e: e.dma_start(out=convw[:].rearrange("p a b -> p (a b)"), in_=convw_d), [], ["convw"])
            DSP(lambda e: e.dma_start(out=alog[:], in_=alog_d.partition_broadcast(128)), [], ["alog"])
            DSP(lambda e: e.dma_start(out=dtb[:], in_=dtb_d.partition_broadcast(128)), [], ["dtb"])
            DSP(lambda e: e.dma_start(out=dnw[:], in_=dnw_d.partition_broadcast(128)), [], ["dnw"])
            DSP(lambda e: e.dma_start(out=rpb_sb[:].rearrange("p a b c -> p (a b c)"),
                                      in_=rpb_d.partition_broadcast(64)), [], ["rpb_sb"])
            POOL(lambda e: e.memset(epsc[:], EPS), [], ["epsc"])
            DVE(lambda e: e.tensor_copy(ident_bf[:], cst[:, 0, :]), ["cst"], ["ident_bf"])
            DVE(lambda e: e.tensor_copy(ones_bf[:], cst[:, 8, :]), ["cst"], ["ones_bf"])
            ACT(lambda e: e.activation(negA[:], alog[:], AF.Exp), ["alog"], ["negA0"])
            DVE(lambda e: e.tensor_scalar(negA[:], negA[:], -1.0, None, ALU.mult), ["negA0"], ["negA"])
            if "N" in phases or "n" in phases:
                POOL(lambda e: e.memset(negt[:], NEG), [], ["negt"])
                for h in range(8):
                    DSP(lambda e, h=h: e.dma_start(out=tb_d[:, h, :, :].rearrange("p a b -> p (a b)"), in_=negt[:]),
                        ["negt"], ["tb%d" % h])
                for h in range(8):
                    dst = bass.AP(tb_h, 8 * 7680 + h * 960, [[7681, 49], [64, 15], [1, 16]])
                    DSP(lambda e, h=h, dst=dst: e.dma_start(out=dst, in_=rpb_sb[8:57, h, :, 7:23]),
                        ["rpb_sb", "tb%d" % h], ["tb%d" % h])
                for c in list(range(0, 8)) + list(range(57, 64)):
                    cs = min(max(c - 8, 0), 48)
                    off = cs - c + 15
                    for h in range(8):
                        DSP(lambda e, c=c, cs=cs, off=off, h=h: e.dma_start(
                            out=tb_d[c:c + 1, h, :, cs:cs + 16], in_=rpb_sb[c:c + 1, h, :, off:off + 16]),
                            ["rpb_sb", "tb%d" % h], ["tb%d" % h])
            S.emit()

        def norm_transpose(stk, src_of_tile, nw, stats_from_ssall):
            xts = Ring([sbt(stk, "nt_x%d" % i, [128, D], F32) for i in range(4)], "nt_x")
            xss = Ring([sbt(stk, "nt_s%d" % i, [128, D], BF16) for i in range(3)], "nt_s")
            junk = sbt(stk, "nt_junk", [128, D], BF16)
            tps = Ring([pst(stk, "nt_tp%d" % i, [128, 8, 128], BF16) for i in range(4)], "nt_tp")
            for t in range(TT):
                xt, xtk = xts.next()
                xs, xsk = xss.next()
                (DSP if t % 2 == 0 else DPL)(lambda e, xt=xt, t=t: e.dma_start(out=xt[:], in_=src_of_tile(t)), ["srcrows%d" % t], [xtk])
                if not stats_from_ssall:
                    ACT(lambda e, xt=xt, t=t: e.activation(junk[:], xt[:], AF.Square, accum_out=ss_all[:, t:t + 1]),
                        [xtk], ["junk", "ss%d" % t])
                    ACT(lambda e, t=t: e.activation(rstd_all[:, t:t + 1], ss_all[:, t:t + 1], AF.Ln, bias=epsc[:, 0:1],
                                                    scale=1.0 / D), ["ss%d" % t, "epsc"], ["rs0%d" % t])
                    ACT(lambda e, t=t: e.activation(rstd_all[:, t:t + 1], rstd_all[:, t:t + 1], AF.Exp, scale=-0.5),
                        ["rs0%d" % t], ["rstd%d" % t])
                ACT(lambda e, xt=xt, xs=xs, t=t: e.activation(xs[:], xt[:], AF.Copy, scale=rstd_all[:, t:t + 1]),
                    [xtk, "rstd%d" % t], [xsk])
                for half in range(2):
                    tp, tpk = tps.next()
                    for j in range(8):
                        k = half * 8 + j
                        PE(lambda e, tp=tp, xs=xs, j=j, k=k: e.transpose(tp[:, j, :], xs[:, k * 128:(k + 1) * 128],
                                                                         ident_bf[:]),
                           [xsk, "ident_bf"], [tpk])
                    DVE(lambda e, tp=tp, half=half, t=t: e.tensor_tensor(
                        R1[:, half * 8:(half + 1) * 8, t * 128:(t + 1) * 128], tp[:],
                        nw[:, half * 8:(half + 1) * 8].unsqueeze(2).broadcast_to([128, 8, 128]), ALU.mult),
                        [tpk, "nw_mix", "nw_ffn"], ["R1_%d" % t])

        R1all = ["R1_%d" % t for t in range(TT)]

        def gdn_phase(s):
            import os
            CUT = int(os.environ.get("GDN_CUT", "99"))
            NPAIR = int(os.environ.get("GDN_NPAIR", "4"))
            NIT = int(os.environ.get("GDN_NIT", "16"))
            nonlocal R1
            with contextlib.ExitStack() as sd:
                qT2 = sbt(sd, "g_qT2", [128, 2, L], BF16)
                kT2 = sbt(sd, "g_kT2", [128, 2, L], BF16)
                vT2 = sbt(sd, "g_vT2", [128, 2, L], BF16)
                zs = sbt(sd, "g_zs", [128, TT, 256], BF16)
                ostore = sbt(sd, "g_ostore", [128, 8, 4, 128], F32)
                S4 = [sbt(sd, "g_S%d" % i, [128, 4, 128], F32) for i in range(2)]
                for hp in range(NPAIR):
                    with contextlib.ExitStack() as sp_:
                        wch = Ring([sbt(sp_, "g_wch%d" % i, [128, KT, 128], BF16) for i in range(6)], "g_wch")
                        wz = sbt(sp_, "g_wz", [128, KT, 256], BF16)
                        cpads = Ring([sbt(sp_, "g_cpad%d" % i, [128, L + 4], F32) for i in range(2)], "g_cpadR")
                        posts = Ring([sbt(sp_, "g_post%d" % i, [128, L], F32) for i in range(2)], "g_postR")
                        sq = sbt(sp_, "g_sq", [128, L], BF16)
                        rnb = Ring([sbt(sp_, "g_rnb%d" % i, [128, 512], F32) for i in range(2)], "g_rnb")
                        pcb = Ring([pst(sp_, "g_pc%d" % i, [128, 512], F32) for i in range(4)], "g_pc")
                        pzb = Ring([pst(sp_, "g_pz%d" % i, [128, 2, 256], F32) for i in range(2)], "g_pz")
                        R1 = sbt(sp_, "R1s", [128, KT, L], BF16)
                        for k in range(KT):
                            (DSP if k % 2 == 0 else DPL)(lambda e, k=k: e.dma_start(out=R1[:, k, :], in_=xn_s[k]), ["xn_s%d" % k], ["R1re_%d" % k])
                        R1deps = ["R1re_%d" % k for k in range(KT)]
                        for ci in range(2):
                            POOL(lambda e, ci=ci: e.memset(cpads.tiles[ci][:, 0:2], 0.0), [], ["g_cpadR%d" % ci])
                            POOL(lambda e, ci=ci: e.memset(cpads.tiles[ci][:, L + 2:L + 4], 0.0), ["g_cpadR%d" % ci], ["g_cpadR%d" % ci])
                        DPL(lambda e, hp=hp: e.dma_start(
                            out=wz[:], in_=w_in[:, OFF_Z + hp * 256: OFF_Z + (hp + 1) * 256].rearrange("(k p) c -> p k c", p=128)),
                            [], ["g_wz"])
                        pend_l2 = []
                        wts = []
                        for kind in range(3):
                            for hh in range(2):
                                col = OFF_DN + kind * 1024 + (2 * hp + hh) * 128
                                wt, wk_ = wch.next()
                                DPL(lambda e, wt=wt, col=col: e.dma_start(
                                    out=wt[:], in_=w_in[:, col:col + 128].rearrange("(k p) c -> p k c", p=128)), [], [wk_])
                                wts.append((wt, wk_))

                        def l2_tail(kind, hh, post, kpo):
                            dst = qT2 if kind == 0 else kT2
                            dk_ = "g_qT2" if kind == 0 else "g_kT2"
                            scl = float(HD ** -0.5) if kind == 0 else 1.0
                            for tg in range(4):
                                pc, pck = pcb.next()
                                rn, rnk = rnb.next()
                                PE(lambda e, pc=pc, tg=tg: e.matmul(pc[:], ones_bf[:], sq[:, tg * 512:(tg + 1) * 512],
                                                                    start=True, stop=True), ["g_sq", "ones_bf"], [pck])
                                ACT(lambda e, pc=pc, rn=rn: e.activation(rn[:], pc[:], AF.Ln, bias=epsc[:, 0:1]), [pck, "epsc"], [rnk])
                                ACT(lambda e, rn=rn: e.activation(rn[:], rn[:], AF.Exp, scale=-0.5), [rnk], [rnk])
                                DVE(lambda e, rn=rn, tg=tg, dst=dst, hh=hh, scl=scl, post=post: e.scalar_tensor_tensor(
                                    dst[:, hh, tg * 512:(tg + 1) * 512], post[:, tg * 512:(tg + 1) * 512], scl, rn[:],
                                    ALU.mult, ALU.mult), [kpo, rnk], [dk_])

                        for kind in range(3):
                            for hh in range(2):
                                h = 2 * hp + hh
                                col = OFF_DN + kind * 1024 + h * 128
                                cidx = kind * 8 + h
                                wt, wk_ = wts[kind * 2 + hh]
                                cpad, kcp = cpads.next()
                                post, kpo = posts.next()
                                for tg in range(4):
                                    pc, pck = pcb.next()
                                    for k in range(KT):
                                        PE(lambda e, pc=pc, k=k, tg=tg, wt=wt: e.matmul(
                                            pc[:], wt[:, k, :], R1[:, k, tg * 512:(tg + 1) * 512], start=(k == 0), stop=(k == KT - 1)),
                                           R1deps + [wk_], [pck])
                                    ACT(lambda e, pc=pc, tg=tg, cpad=cpad: e.copy(cpad[:, 2 + tg * 512: 2 + (tg + 1) * 512], pc[:]),
                                        [pck], [kcp])
                                if pend_l2:
                                    l2_tail(*pend_l2.pop(0))
                                DVE(lambda e, cidx=cidx, post=post, cpad=cpad: e.tensor_scalar(post[:], cpad[:, 0:L], convw[:, cidx, 0:1], None, ALU.mult),
                                    [kcp, "convw"], [kpo])
                                for j in range(1, 5):
                                    DVE(lambda e, cidx=cidx, j=j, post=post, cpad=cpad: e.scalar_tensor_tensor(
                                        post[:], cpad[:, j:j + L], convw[:, cidx, j:j + 1], post[:], ALU.mult, ALU.add),
                                        [kcp, "convw", kpo], [kpo])
                                if kind == 2:
                                    ACT(lambda e, hh=hh, post=post: e.activation(vT2[:, hh, :], post[:], AF.Silu), [kpo], ["g_vT2"])
                                else:
                                    ACT(lambda e, post=post: e.activation(post[:], post[:], AF.Silu), [kpo], [kpo])
                                    ACT(lambda e, post=post: e.activation(sq[:], post[:], AF.Square), [kpo], ["g_sq"])
                                    pend_l2.append((kind, hh, post, kpo))
                        while pend_l2:
                            kind_, hh_, post, kpo = pend_l2.pop(0)
                            l2_tail(kind_, hh_, post, kpo)
                        if False:
                            if False:
                                if False:
                                    for tg in range(4):
                                        pc, pck = pcb.next()
                                        rn, rnk = rnb.next()
                                        PE(lambda e, pc=pc, tg=tg: e.matmul(pc[:], ones_bf[:], sq[:, tg * 512:(tg + 1) * 512],
                                                                            start=True, stop=True), ["g_sq", "ones_bf"], [pck])
                                        ACT(lambda e, pc=pc, rn=rn: e.activation(rn[:], pc[:], AF.Ln, bias=epsc[:, 0:1]), [pck, "epsc"], [rnk])
                                        ACT(lambda e, rn=rn: e.activation(rn[:], rn[:], AF.Exp, scale=-0.5), [rnk], [rnk])
                                        DVE(lambda e, rn=rn, tg=tg, dst=dst, hh=hh, scl=scl, post=post: e.scalar_tensor_tensor(
                                            dst[:, hh, tg * 512:(tg + 1) * 512], post[:, tg * 512:(tg + 1) * 512], scl, rn[:],
                                            ALU.mult, ALU.mult), [kpo, rnk], [dk_])
                        for t2 in range(TT // 2):
                            pz, pzk = pzb.next()
                            for tl in range(2):
                                t = t2 * 2 + tl
                                for k in range(KT):
                                    PE(lambda e, pz=pz, tl=tl, t=t, k=k: e.matmul(
                                        pz[:, tl, :], R1[:, k, t * 128:(t + 1) * 128], wz[:, k, :], start=(k == 0), stop=(k == KT - 1)),
                                       R1deps + ["g_wz"], [pzk])
                            ACT(lambda e, pz=pz, t2=t2: e.activation(zs[:, t2 * 2:(t2 + 1) * 2, :], pz[:], AF.Silu), [pzk], ["g_zs"])
                        S.emit()
                    with contextlib.ExitStack() as si:
                        sets = []
                        NSETS = int(os.environ.get("GDN_NSETS", "4"))
                        for q in range(NSETS):
                            B = {}
                            for bi_, nm in enumerate(("b1", "b2", "b3", "b4", "b5", "b6", "b7", "b8", "b9", "b10", "b11", "b12")):
                                B[nm] = sbt(si, "g_%s_%d" % (nm, q), [128, 4, 128], F32)
                            B["R4"] = sbt(si, "g_R4_%d" % q, [128, 4, 256], BF16)
                            B["UW4"] = sbt(si, "g_UW4_%d" % q, [128, 4, 256], BF16)
                            B["qkTb"] = sbt(si, "g_qkTb_%d" % q, [128, 4, 128], BF16)
                            B["kdb"] = sbt(si, "g_kdb_%d" % q, [128, 4, 128], BF16)
                            B["TTb"] = sbt(si, "g_TTb_%d" % q, [128, 4, 128], BF16)
                            B["yT4"] = sbt(si, "g_yT4_%d" % q, [128, 4, 128], BF16)
                            B["y4"] = sbt(si, "g_y4_%d" % q, [128, 4, 128], BF16)
                            B["gs"] = sbt(si, "g_gs_%d" % q, [128, 3, 4], F32)
                            B["bt"] = sbt(si, "g_bt_%d" % q, [128, 2, 4], F32)
                            B["st4"] = sbt(si, "g_st4_%d" % q, [128, 2, 4], F32)
                            B["Gc4"] = sbt(si, "g_Gc4_%d" % q, [128, 4], F32)
                            sets.append(B)
                        gb = Ring([pst(si, "g_b%d" % i, [128, 4, 128], F32) for i in range(6)], "g_b")
                        gbb = Ring([pst(si, "g_bb%d" % i, [128, 8, 128], BF16) for i in range(2)], "g_bb")
                        POOL(lambda e: e.memset(S4[0][:], 0.0), [], ["g_S0"])
                        NEG4c = sbt(si, "g_NEG4c", [128, 4, 128], F32)
                        for d_ in range(2):
                            POOL(lambda e, d_=d_: e.tensor_copy(NEG4c[:, 2 * d_:2 * d_ + 2, :],
                                                                cst[:, 5 + d_, :].unsqueeze(1).broadcast_to([128, 2, 128])),
                                 ["cst"], ["g_NEG4c"])
                        GM = [cst[:, 1, :], cst[:, 2, :]]
                        GD = [cst[:, 3, :], cst[:, 4, :]]
                        NG = [cst[:, 5, :], cst[:, 6, :]]

                        def iteration(n, q):
                            B = sets[q]
                            K_ = lambda nm: "g_%s_%d" % (nm, q)
                            Mg4, kd4 = B["b1"], B["kdb"]
                            D4, Wn4 = B["b2"], B["b2"]
                            E4, Se4 = B["b3"], B["b3"]
                            Ds4, N4 = B["b4"], B["b4"]
                            A4, os4 = B["b5"], B["b5"]
                            AT4, sq4 = B["b6"], B["b6"]
                            qk4, qt4 = B["b7"], B["b7"]
                            qkT4 = B["qkTb"]
                            Pb_, PTb_, Xa, Xb = B["b9"], B["b10"], B["b11"], B["b12"]
                            R4, UW4, y4, gs, bt, st4 = B["R4"], B["UW4"], B["y4"], B["gs"], B["bt"], B["st4"]
                            M4, Gc4, kgc = B["b9"], B["Gc4"], K_("Gc4")
                            kMg, kD, kE, kDs, kA, kAT, kqk, kqkT = K_("b1"), K_("b2"), K_("b3"), K_("b4"), K_("b5"), K_("b6"), K_("b7"), K_("b8")
                            kkd, kWn, kSe, kN, kos, ksq, kqt = K_("kdb"), kD, kE, kDs, kA, kAT, kqk
                            kqkT = K_("qkTb")
                            TTb, kTTb = B["TTb"], K_("TTb")
                            kR4, kUW, ky4, kgs, kbt, kst = K_("R4"), K_("UW4"), K_("y4"), K_("gs"), K_("bt"), K_("st4")
                            chunk = [n, 15 - n]
                            gc0 = [2 * hp, 8 + 2 * hp]
                            cs_ = lambda d: slice(chunk[d] * 128, (chunk[d] + 1) * 128)
                            b_, bk_ = gb.next()
                            gsp = b_[:, 0, 0:12].rearrange("p (a b) -> p a b", a=3)
                            for d in range(2):
                                for kind, msk in enumerate((GM[d], GD[d], ones_f)):
                                    PE(lambda e, gsp=gsp, kind=kind, d=d, msk=msk, c=chunk[d], g0=gc0[d]: e.matmul(
                                        gsp[:, kind, 2 * d:2 * d + 2], msk, g_all[:, c, g0:g0 + 2], start=True, stop=True),
                                       ["g_all", "cst"], [bk_])
                            ACT(lambda e, gsp=gsp: e.copy(Gc4[:], gsp[:, 0, :]), [bk_], [kgc])
                            ACT(lambda e, gsp=gsp: e.activation(gs[:], gsp, AF.Exp), [bk_], [kgs])
                            for d in range(2):
                                DVE(lambda e, d=d, c=chunk[d], g0=gc0[d]: e.tensor_copy(bt[:, 0, 2 * d:2 * d + 2], beta_all[:, c, g0:g0 + 2]),
                                    ["beta_all"], [kbt])
                            DVE(lambda e: e.tensor_tensor(bt[:, 1, :], bt[:, 0, :], gs[:, 0, :], ALU.mult), [kbt, kgs], [kbt])
                            for d in range(2):
                                DVE(lambda e, d=d, c=chunk[d], g0=gc0[d]: e.tensor_tensor(
                                    Mg4[:, 2 * d:2 * d + 2, :], GM[d].unsqueeze(1).broadcast_to([128, 2, 128]),
                                    g_all[:, c, g0:g0 + 2].unsqueeze(2).broadcast_to([128, 2, 128]), ALU.mult),
                                    ["g_all", "cst"], [kMg])
                            yield
                            POOL(lambda e: e.tensor_tensor(M4[:], NEG4c[:], Gc4[:].unsqueeze(2).broadcast_to([128, 4, 128]), ALU.add),
                                 [kgc, "g_NEG4c"], [K_("b9")])
                            b_, bk_ = gb.next()
                            for s_ in range(4):
                                PE(lambda e, b_=b_, s_=s_: e.matmul(b_[:, s_, :], ones_f, Mg4[:, s_, :], start=True, stop=True),
                                   [kMg, "cst"], [bk_])
                            ACT(lambda e, b_=b_: e.activation(E4[:], b_[:], AF.Exp), [bk_], [kE])
                            DVE(lambda e, b_=b_: e.scalar_tensor_tensor(D4[:], b_[:], -1.0, M4[:], ALU.mult, ALU.add),
                                [bk_, kE, K_("b9")], [kD])
                            ACT(lambda e: e.activation(D4[:], D4[:], AF.Exp), [kD], [kD])
                            bkk, bkkk = gb.next()
                            bqk, bqkk = gb.next()
                            for s_ in range(4):
                                d, hh = s_ // 2, s_ % 2
                                PE(lambda e, bkk=bkk, s_=s_, hh=hh, c=cs_(d): e.matmul(bkk[:, s_, :], kT2[:, hh, c], kT2[:, hh, c],
                                                                                  start=True, stop=True), ["g_kT2"], [bkkk])
                            for s_ in range(4):
                                d, hh = s_ // 2, s_ % 2
                                PE(lambda e, bqk=bqk, s_=s_, hh=hh, c=cs_(d): e.matmul(bqk[:, s_, :], qT2[:, hh, c], kT2[:, hh, c],
                                                                                  start=True, stop=True), ["g_kT2", "g_qT2"], [bqkk])
                            POOL(lambda e: e.tensor_tensor(Ds4[:], D4[:], cst[:, 7, :].unsqueeze(1).broadcast_to([128, 4, 128]), ALU.mult),
                                 [kD, "cst"], [kDs])
                            POOL(lambda e: e.tensor_tensor(Ds4[:], Ds4[:], bt[:, 0, :].unsqueeze(2).broadcast_to([128, 4, 128]), ALU.mult),
                                 [kDs, kbt], [kDs])
                            DVE(lambda e, bkk=bkk: e.tensor_tensor(A4[:], bkk[:], Ds4[:], ALU.mult), [bkkk, kDs], [kA])
                            DVE(lambda e, bqk=bqk: e.tensor_tensor(qk4[:], bqk[:], D4[:], ALU.mult), [bqkk, kD], [kqk])
                            yield
                            b_, bk_ = gb.next()
                            for s_ in range(4):
                                PE(lambda e, b_=b_, s_=s_: e.transpose(b_[:, s_, :], A4[:, s_, :], ident_f), [kA, "cst"], [bk_])
                            ACT(lambda e, b_=b_: e.copy(AT4[:], b_[:]), [bk_], [kAT])
                            DVE(lambda e: e.tensor_tensor(Xa[:], ident_f.unsqueeze(1).broadcast_to([128, 4, 128]), AT4[:], ALU.subtract),
                                [kAT, "cst"], [K_("b11")])
                            b_, bk_ = gb.next()
                            for s_ in range(4):
                                PE(lambda e, b_=b_, s_=s_: e.transpose(b_[:, s_, :], qk4[:, s_, :], ident_f), [kqk, "cst"], [bk_])
                            ACT(lambda e, b_=b_: e.copy(qkT4[:], b_[:]), [bk_], [kqkT])
                            yield
                            Pp, Ppk, PTp, PTpk = A4, kA, AT4, kAT
                            Pn_, Pnk, PTn, PTnk = Pb_, K_("b9"), PTb_, K_("b10")
                            Xp, Xpk, Xn, Xnk = Xa, K_("b11"), Xb, K_("b12")
                            for m in range(1, 7):
                                b_, bk_ = gb.next()
                                for s_ in range(4):
                                    PE(lambda e, b_=b_, s_=s_, PTp=PTp, Pp=Pp: e.matmul(b_[:, s_, :], PTp[:, s_, :], Pp[:, s_, :],
                                                                                     start=True, stop=True), [Ppk, PTpk], [bk_])
                                ACT(lambda e, b_=b_, Pn_=Pn_: e.copy(Pn_[:], b_[:]), [bk_], [Pnk])
                                if m < 6:
                                    yield
                                    b2, b2k = gb.next()
                                    for s_ in range(4):
                                        PE(lambda e, b2=b2, s_=s_, Pn_=Pn_: e.transpose(b2[:, s_, :], Pn_[:, s_, :], ident_f), [Pnk, "cst"], [b2k])
                                    ACT(lambda e, b2=b2, PTn=PTn: e.copy(PTn[:], b2[:]), [b2k], [PTnk])
                                yield
                                b3, b3k = gb.next()
                                for s_ in range(4):
                                    PE(lambda e, b3=b3, s_=s_, Pn_=Pn_, Xp=Xp: e.matmul(b3[:, s_, :], Pn_[:, s_, :], Xp[:, s_, :],
                                                                                     start=True, stop=True), [Pnk, Xpk], [b3k])
                                DVE(lambda e, b3=b3, Xn=Xn, Xp=Xp: e.tensor_tensor(Xn[:], Xp[:], b3[:], ALU.add), [b3k, Xpk], [Xnk])
                                yield
                                Pp, Ppk, Pn_, Pnk = Pn_, Pnk, Pp, Ppk
                                PTp, PTpk, PTn, PTnk = PTn, PTnk, PTp, PTpk
                                Xp, Xpk, Xn, Xnk = Xn, Xnk, Xp, Xpk
                            ACT(lambda e, Xp=Xp: e.copy(TTb[:], Xp[:]), [Xpk], [kTTb])
                            TT_, TTk = TTb, kTTb
                            bb_, bbk = gbb.next()
                            for d in range(2):
                                for hh in range(2):
                                    for kv, src, srck in ((0, kT2, "g_kT2"), (1, vT2, "g_vT2")):
                                        PE(lambda e, bb_=bb_, sl=d * 4 + hh * 2 + kv, src=src, hh=hh, c=cs_(d): e.transpose(
                                            bb_[:, sl, :], src[:, hh, c], ident_bf[:]), [srck, "ident_bf"], [bbk])
                            for d in range(2):
                                kview = bb_[:, 4 * d:4 * d + 4:2, :]
                                vview = bb_[:, 4 * d + 1:4 * d + 4:2, :]
                                DVE(lambda e, d=d, vview=vview: e.tensor_tensor(
                                    R4[:, 2 * d:2 * d + 2, 0:128], vview, bt[:, 0, 2 * d:2 * d + 2].unsqueeze(2).broadcast_to([128, 2, 128]),
                                    ALU.mult), [bbk, kbt], [kR4])
                                DVE(lambda e, d=d, kview=kview: e.tensor_tensor(
                                    R4[:, 2 * d:2 * d + 2, 128:256], kview, bt[:, 1, 2 * d:2 * d + 2].unsqueeze(2).broadcast_to([128, 2, 128]),
                                    ALU.mult), [bbk, kbt, kR4], [kR4])
                                DVE(lambda e, d=d, kview=kview: e.tensor_tensor(
                                    kd4[:, 2 * d:2 * d + 2, :], kview, gs[:, 1, 2 * d:2 * d + 2].unsqueeze(2).broadcast_to([128, 2, 128]),
                                    ALU.mult), [bbk, kgs], [kkd])
                            yield
                            for half in range(2):
                                b_, bk_ = gb.next()
                                bv = b_[:].rearrange("p a b -> p (a b)").rearrange("p (a b) -> p a b", a=2)
                                for sl in range(2):
                                    s_ = half * 2 + sl
                                    PE(lambda e, bv=bv, sl=sl, s_=s_, TT_=TT_: e.matmul(bv[:, sl, :], TT_[:, s_, :], R4[:, s_, :],
                                                                                     start=True, stop=True), [TTk, kR4], [bk_])
                                if half == 0:
                                    ACT(lambda e, bv=bv: e.copy(UW4[:, 0:2, :], bv), [bk_], [kUW])
                                else:
                                    DVE(lambda e, bv=bv: e.tensor_copy(UW4[:, 2:4, :], bv), [bk_], [kUW])
                            yield
                            bw, bwk = gb.next()
                            for s_ in range(4):
                                PE(lambda e, bw=bw, s_=s_: e.matmul(bw[:, s_, :], UW4[:, s_, 128:256], kd4[:, s_, :], start=True, stop=True),
                                   [kUW, kkd], [bwk])
                            bq, bqk_ = gb.next()
                            for s_ in range(4):
                                PE(lambda e, bq=bq, s_=s_: e.matmul(bq[:, s_, :], UW4[:, s_, 128:256], qkT4[:, s_, :], start=True, stop=True),
                                   [kUW, kqkT], [bqk_])
                            ACT(lambda e, bw=bw: e.activation(Wn4[:], bw[:], AF.Copy, scale=-1.0), [bwk], [kWn])
                            for d in range(2):
                                DVE(lambda e, d=d, c=cs_(d): e.tensor_tensor(qt4[:, 2 * d:2 * d + 2, :], qT2[:, :, c], E4[:, 2 * d:2 * d + 2, :],
                                                                            ALU.mult), ["g_qT2", kE], [kqt])
                            DVE(lambda e, bq=bq: e.tensor_tensor(qt4[:], qt4[:], bq[:], ALU.subtract), [bqk_, kqt], [kqt])
                            yield
                            So, Sok = S4[n % 2], "g_S%d" % (n % 2)
                            Sn, Snk = S4[(n + 1) % 2], "g_S%d" % ((n + 1) % 2)
                            bo, bok = gb.next()
                            for s_ in range(4):
                                PE(lambda e, bo=bo, s_=s_: e.matmul(bo[:, s_, :], qkT4[:, s_, :], UW4[:, s_, 0:128], start=True, stop=False),
                                   [kqkT, kUW], [bok])
                                PE(lambda e, bo=bo, s_=s_, So=So: e.matmul(bo[:, s_, :], qt4[:, s_, :], So[:, s_, :], start=False, stop=True),
                                   [kqt, Sok], [bok])
                            bs, bsk = gb.next()
                            for s_ in range(4):
                                PE(lambda e, bs=bs, s_=s_: e.matmul(bs[:, s_, :], kd4[:, s_, :], UW4[:, s_, 0:128], start=True, stop=False),
                                   [kUW, kkd], [bsk])
                                PE(lambda e, bs=bs, s_=s_, So=So: e.matmul(bs[:, s_, :], Wn4[:, s_, :], So[:, s_, :], start=False, stop=True),
                                   [kWn, Sok], [bsk])
                            POOL(lambda e, So=So: e.tensor_tensor(Se4[:], So[:], gs[:, 2, :].unsqueeze(2).broadcast_to([128, 4, 128]), ALU.mult),
                                 [Sok, kgs, kE], [kSe])
                            DVE(lambda e, bs=bs, Sn=Sn: e.tensor_tensor(Sn[:], Se4[:], bs[:], ALU.add), [bsk, kSe], [Snk])
                            if n < 8:
                                ACT(lambda e, bo=bo, n=n: e.copy(ostore[:, n, :, :], bo[:]), [bok], ["g_ostore%d" % n])
                            else:
                                m_ = 15 - n
                                DVE(lambda e, bo=bo, m_=m_: e.tensor_tensor(os4[:, 0:2, :], bo[:, 0:2, :], ostore[:, m_, 2:4, :], ALU.add),
                                    [bok, "g_ostore%d" % m_], [kos])
                                DVE(lambda e, bo=bo, m_=m_: e.tensor_tensor(os4[:, 2:4, :], bo[:, 2:4, :], ostore[:, m_, 0:2, :], ALU.add),
                                    [bok, "g_ostore%d" % m_, kos], [kos])
                                ACT(lambda e: e.activation(sq4[:], os4[:], AF.Square), [kos], [ksq])
                                DVE(lambda e: e.reduce_sum(st4[:, 0, :], sq4[:], AX.X), [ksq], [kst])
                                ACT(lambda e: e.activation(st4[:, 1, :], st4[:, 0, :], AF.Ln, bias=epsc[:, 0:1], scale=1.0 / HD), [kst, "epsc"], [kst])
                                ACT(lambda e: e.activation(st4[:, 1, :], st4[:, 1, :], AF.Exp, scale=-0.5), [kst], [kst])
                                DVE(lambda e: e.tensor_tensor(os4[:], os4[:], st4[:, 1, :].unsqueeze(2).broadcast_to([128, 4, 128]), ALU.mult),
                                    [kos, kst], [kos])
                                DVE(lambda e: e.tensor_tensor(os4[:], os4[:], dnw[:].unsqueeze(1).broadcast_to([128, 4, 128]), ALU.mult),
                                    [kos, "dnw"], [kos])
                                for d in range(2):
                                    DVE(lambda e, d=d, c=chunk[d]: e.tensor_tensor(
                                        y4[:, 2 * d:2 * d + 2, :], os4[:, 2 * d:2 * d + 2, :],
                                        zs[:, c, :].rearrange("p (a b) -> p a b", a=2), ALU.mult), [kos, "g_zs"], [ky4])
                                def ytail(y4=y4, ky4=ky4, yT4=B["yT4"], kyT=K_("yT4"), cs0=cs_(0), cs1=cs_(1)):
                                    bb_, bbk = gbb.next()
                                    for s_ in range(4):
                                        PE(lambda e, bb_=bb_, s_=s_: e.transpose(bb_[:, s_, :], y4[:, s_, :], ident_bf[:]), [ky4, "ident_bf"], [bbk])
                                    ACT(lambda e, bb_=bb_: e.copy(yT4[:], bb_[:, 0:4, :]), [bbk], [kyT])
                                    for d, c in ((0, cs0), (1, cs1)):
                                        DSP(lambda e, d=d, c=c: e.dma_start(
                                            out=mix_d[s, 1024 + hp * 256: 1024 + (hp + 1) * 256, c].rearrange("(a p) l -> p a l", p=128),
                                            in_=yT4[:, 2 * d:2 * d + 2, :]), [kyT], ["mix_d"])
                                deferred.append(ytail)
                            yield

                        deferred = []
                        for n0 in range(0, NIT, NSETS):
                            gens = [iteration(n0 + i_, i_) for i_ in range(min(NSETS, NIT - n0))]
                            alive = [True] * len(gens)
                            stepi = 0
                            while any(alive):
                                for gi in range(len(gens)):
                                    if alive[gi]:
                                        try:
                                            next(gens[gi])
                                        except StopIteration:
                                            alive[gi] = False
                                stepi += 1
                                if stepi in (3, 5, 7, 9) and deferred:
                                    deferred.pop(0)()
                        while deferred:
                            deferred.pop(0)()
                        S.emit()

        for s in range(nslot):
            def gates_prefetch(sg):
                wba = sbt(sg, "wba", [128, KT, 32], BF16)
                DPL(lambda e: e.dma_start(out=wba[:], in_=w_in[:, OFF_B:OFF_B + 32].rearrange("(k p) c -> p k c", p=128)),
                    [], ["wba"])
                return wba

            def gates_phase(sg, wba):
                if True:
                    ba = sbt(sg, "ba", [128, TT, 32], F32)
                    tmpg = sbt(sg, "tmpg", [128, TT, 16], F32)
                    gp = pst(sg, "gp", [128, TT, 32], F32)
                    for t in range(TT):
                        for k in range(KT):
                            PE(lambda e, t=t, k=k: e.matmul(gp[:, t, :], R1[:, k, t * 128:(t + 1) * 128], wba[:, k, :],
                                                            start=(k == 0), stop=(k == KT - 1)),
                               ["R1_%d" % t, "wba"], ["gp"])
                    ACT(lambda e: e.copy(ba[:], gp[:]), ["gp"], ["ba"])
                    ACT(lambda e: e.activation(tmpg[:], ba[:, :, 0:16], AF.Exp, scale=-1.0), ["ba"], ["tmpg"])
                    DVE(lambda e: e.tensor_scalar(tmpg[:], tmpg[:], 1.0, None, ALU.add), ["tmpg"], ["tmpg"])
                    DVE(lambda e: e.reciprocal(beta_all[:], tmpg[:]), ["tmpg"], ["beta_all"])
                    DVE(lambda e: e.tensor_tensor(tmpg[:], ba[:, :, 16:32],
                                                  dtb[:].unsqueeze(1).broadcast_to([128, TT, 16]), ALU.add),
                        ["ba", "dtb", "beta_all"], ["tmpg"])
                    ACT(lambda e: e.activation(tmpg[:], tmpg[:], AF.Exp), ["tmpg"], ["tmpg"])
                    ACT(lambda e: e.activation(tmpg[:], tmpg[:], AF.Ln, bias=1.0), ["tmpg"], ["tmpg"])
                    DVE(lambda e: e.tensor_tensor(g_all[:], tmpg[:], negA[:].unsqueeze(1).broadcast_to([128, TT, 16]),
                                                  ALU.mult), ["tmpg", "negA"], ["g_all"])

            sR = contextlib.ExitStack()
            R1 = sbt(sR, "R1a", [128, KT, L], BF16)
            if "A" in phases:
                with contextlib.ExitStack() as sa:
                    wba_ = gates_prefetch(sa) if "G" in phases else None
                    norm_transpose(sa, lambda t: x[s, t * 128:(t + 1) * 128, :], nw_mix, False)
                    for k in range(KT):
                        DSP(lambda e, k=k: e.dma_start(out=xn_s[k], in_=R1[:, k, :]), R1all, ["xn_s%d" % k])
                    if "G" in phases:
                        gates_phase(sa, wba_)
                    S.emit()

            if "N" in phases:
                with contextlib.ExitStack() as sn:
                    wq = Ring([sbt(sn, "wq%d" % i, [128, KT, 128], BF16) for i in range(2)], "wq")
                    wk = Ring([sbt(sn, "wk%d" % i, [128, KT, 128], BF16) for i in range(2)], "wk")
                    wv = Ring([sbt(sn, "wv%d" % i, [128, KT, 128], BF16) for i in range(2)], "wv")
                    qTs = [sbt(sn, "na_qT%d" % i, [128, L], BF16) for i in range(2)]
                    kTs = [sbt(sn, "na_kT%d" % i, [128, L], BF16) for i in range(2)]
                    vvs = [sbt(sn, "na_v%d" % i, [128, TT, 128], BF16) for i in range(2)]
                    Bms = [sbt(sn, "na_Bm%d" % i, [128, 5, 640], F32) for i in range(2)]
                    Sb = Ring([sbt(sn, "na_Sb%d" % i, [128, 640], F32) for i in range(2)], "na_Sb")
                    Pb = Ring([sbt(sn, "na_P%d" % i, [128, 640], BF16) for i in range(2)], "na_P")
                    Pn = Ring([sbt(sn, "na_Pn%d" % i, [128, 640], BF16) for i in range(2)], "na_Pn")
                    PTs = Ring([sbt(sn, "na_PT%d" % i, [128, 5, 128], BF16) for i in range(2)], "na_PT")
                    st8 = Ring([sbt(sn, "na_st%d" % i, [128, 4], F32) for i in range(2)], "na_st")
                    ona = Ring([sbt(sn, "na_o%d" % i, [128, L], BF16) for i in range(2)], "na_o")
                    psA = Ring([pst(sn, "na_psA%d" % i, [128, 1024], F32) for i in range(2)], "na_psA")
                    ptp = Ring([pst(sn, "na_ptp%d" % i, [128, 8, 128], BF16) for i in range(2)], "na_ptp")
                    pin = Ring([pst(sn, "na_pin%d" % i, [128, 512], F32) for i in range(2)], "na_pin")
                    for bi in range(2):
                        POOL(lambda e, bi=bi: e.memset(Bms[bi][:], NEG), [], ["Bm%d_%d_%d" % (bi, a_, b_) for a_ in range(5) for b_ in range(2)])
                    cls_tab = [((0, 7), (0, 6)), ((0, 5), (0, 4)), ((0, 3), (1, 3)), ((2, 3), (2, 2)), ((2, 1), (2, 0))]

                    def inproj(h):
                        bi = h % 2
                        qT, kT, vv, Bm = qTs[bi], kTs[bi], vvs[bi], Bms[bi]
                        wqt, wqk = wq.next()
                        wkt, wkk = wk.next()
                        wvt, wvk = wv.next()
                        for (wt, wkey, off) in ((wqt, wqk, 0), (wkt, wkk, 1024), (wvt, wvk, 2048)):
                            DPL(lambda e, wt=wt, off=off, h=h: e.dma_start(
                                out=wt[:], in_=w_in[:, off + h * 128: off + (h + 1) * 128].rearrange("(k p) c -> p k c", p=128)),
                                [], [wkey])
                        for cl in range(5):
                            for e_ in range(2):
                                j0, dr0 = cls_tab[cl][e_]
                                DSP(lambda e, cl=cl, e_=e_, j0=j0, dr0=dr0, h=h, Bm=Bm: e.dma_start(
                                    out=Bm[e_ * 64:(e_ + 1) * 64, cl, j0 * 64:(j0 + 8) * 64],
                                    in_=tb_d[:, h, dr0:dr0 + 8, :].rearrange("p a b -> p (a b)")),
                                    ["tb%d" % h], ["Bm%d_%d_%d" % (bi, cl, e_)])
                        yield
                        for tg in range(4):
                            pq, pqk = pin.next()
                            for k in range(KT):
                                PE(lambda e, pq=pq, k=k, tg=tg, wqt=wqt: e.matmul(
                                    pq[:], wqt[:, k, :], R1[:, k, tg * 512:(tg + 1) * 512], start=(k == 0), stop=(k == KT - 1)),
                                   R1all + [wqk], [pqk])
                            ACT(lambda e, pq=pq, tg=tg, qT=qT: e.activation(qT[:, tg * 512:(tg + 1) * 512], pq[:], AF.Copy,
                                                                      scale=float(HD ** -0.5)), [pqk], ["na_qT%d" % bi])
                            yield
                            pk, pkk = pin.next()
                            for k in range(KT):
                                PE(lambda e, pk=pk, k=k, tg=tg, wkt=wkt: e.matmul(
                                    pk[:], wkt[:, k, :], R1[:, k, tg * 512:(tg + 1) * 512], start=(k == 0), stop=(k == KT - 1)),
                                   R1all + [wkk], [pkk])
                            DVE(lambda e, pk=pk, tg=tg, kT=kT: e.tensor_copy(kT[:, tg * 512:(tg + 1) * 512], pk[:]), [pkk], ["na_kT%d" % bi])
                            yield
                        for t4 in range(4):
                            pv, pvk = pin.next()
                            for tl in range(4):
                                t = t4 * 4 + tl
                                for k in range(KT):
                                    PE(lambda e, pv=pv, k=k, t=t, tl=tl, wvt=wvt: e.matmul(
                                        pv[:, tl * 128:(tl + 1) * 128], R1[:, k, t * 128:(t + 1) * 128], wvt[:, k, :],
                                        start=(k == 0), stop=(k == KT - 1)), R1all + [wvk], [pvk])
                                if tl == 1:
                                    yield
                            ACT(lambda e, pv=pv, t4=t4, vv=vv: e.copy(vv[:, t4 * 4:(t4 + 1) * 4, :].rearrange("p a b -> p (a b)"), pv[:]),
                                [pvk], ["na_v%d" % bi])
                            yield

                    def unit(h, p, ot, otk):
                        bi = h % 2
                        qT, kT, vv, Bm = qTs[bi], kTs[bi], vvs[bi], Bms[bi]
                        kq, kk_, kv_ = "na_qT%d" % bi, "na_kT%d" % bi, "na_v%d" % bi
                        ts = min(max(p - 2, 0), 11)
                        cl = 0 if p == 0 else 1 if p == 1 else 3 if p == 14 else 4 if p == 15 else 2
                        kbm = ["Bm%d_%d_0" % (bi, cl), "Bm%d_%d_1" % (bi, cl)]
                        pa, pak = psA.next()
                        sbf, sbk = Sb.next()
                        pbt, pbtk = Pb.next()
                        pnt, pnk = Pn.next()
                        ptt, pttk = PTs.next()
                        stt, sttk = st8.next()
                        pp, ppk = ptp.next()
                        PE(lambda e: e.matmul(pa[:, 0:512], qT[:, p * 128:(p + 1) * 128], kT[:, ts * 128: ts * 128 + 512], start=True, stop=True),
                           [kq, kk_], [pak])
                        PE(lambda e: e.matmul(pa[:, 512:640], qT[:, p * 128:(p + 1) * 128], kT[:, ts * 128 + 512: ts * 128 + 640],
                                              start=True, stop=True), [kq, kk_], [pak])
                        yield
                        DVE(lambda e: e.tensor_tensor(sbf[:], pa[:, 0:640], Bm[:, cl, :], ALU.add), [pak] + kbm, [sbk])
                        DVE(lambda e: e.reduce_max(stt[:, 1:2], sbf[:], AX.X, negate=True), [sbk], [sttk])
                        yield
                        ACT(lambda e: e.activation(pbt[:], sbf[:], AF.Exp, bias=stt[:, 1:2], accum_out=stt[:, 2:3]), [sbk, sttk], [pbtk, sttk])
                        yield
                        DVE(lambda e: e.reciprocal(stt[:, 3:4], stt[:, 2:3]), [sttk], [sttk])
                        DVE(lambda e: e.tensor_scalar(pnt[:], pbt[:], stt[:, 3:4], None, ALU.mult), [pbtk, sttk], [pnk])
                        yield
                        for j in range(5):
                            PE(lambda e, j=j: e.transpose(pp[:, j, :], pnt[:, j * 128:(j + 1) * 128], ident_bf[:]), [pnk, "ident_bf"], [ppk])
                        yield
                        ACT(lambda e: e.copy(ptt[:], pp[:, 0:5, :]), [ppk], [pttk])
                        yield
                        for j in range(5):
                            PE(lambda e, j=j: e.matmul(pa[:, 640:768], vv[:, ts + j, :], ptt[:, j, :], start=(j == 0), stop=(j == 4)),
                               [kv_, pttk], [pak])
                        yield
                        ACT(lambda e: e.copy(ot[:, p * 128:(p + 1) * 128], pa[:, 640:768]), [pak], [otk])
                        yield

                    def drive(gens):
                        alive = [True] * len(gens)
                        while any(alive):
                            for gi in range(len(gens)):
                                if alive[gi]:
                                    try:
                                        next(gens[gi])
                                    except StopIteration:
                                        alive[gi] = False

                    def prepass():
                        stg = Ring([sbt(sn, "stg%d" % i, [128, 4096], BF16) for i in range(3)], "stg")
                        for (wsrc, wdst, nm) in ((w_gate, wg_s, "wgs"), (w_up, wu_s, "wus")):
                            for k in range(KT):
                                for pc, (c0, wdt_) in enumerate(((0, 4096), (4096, DFF - 4096))):
                                    sg_, sgk_ = stg.next()
                                    DPL(lambda e, sg_=sg_, wsrc=wsrc, k=k, c0=c0, wdt_=wdt_: e.dma_start(
                                        out=sg_[:, 0:wdt_], in_=wsrc[k * 128:(k + 1) * 128, c0:c0 + wdt_]), [], [sgk_])
                                    DSP(lambda e, sg_=sg_, wdst=wdst, k=k, c0=c0, wdt_=wdt_: e.dma_start(
                                        out=wdst[c0 // 256:(c0 + wdt_) // 256, :, k, :].rearrange("f p c -> p f c"),
                                        in_=sg_[:, 0:wdt_].rearrange("p (f c) -> p f c", c=256)), [sgk_], ["%s_%d_%d" % (nm, k, pc)])
                                    yield
                        for r in range(FC):
                            sg_, sgk_ = stg.next()
                            DPL(lambda e, sg_=sg_, r=r: e.dma_start(out=sg_[:, 0:D], in_=w_down[r * 128:(r + 1) * 128, :]), [], [sgk_])
                            DSP(lambda e, sg_=sg_, r=r: e.dma_start(
                                out=wd_s[r // 4, :, :, r % 4, :].rearrange("g p c -> p g c"),
                                in_=sg_[:, 0:D].rearrange("p (g c) -> p g c", c=512)), [sgk_], ["wds_%d" % r])
                            yield

                    pre = prepass() if s == 0 else iter(())
                    drive([inproj(0)])
                    for h in range(8):
                        ot, otk = ona.next()
                        filler = inproj(h + 1) if h + 1 < 8 else iter(())
                        for p0 in range(0, 16, 2):
                            gens = [unit(h, p0, ot, otk), unit(h, p0 + 1, ot, otk)]
                            alive = [True, True]
                            stepi = 0
                            while any(alive):
                                for gi in range(2):
                                    if alive[gi]:
                                        try:
                                            next(gens[gi])
                                        except StopIteration:
                                            alive[gi] = False
                                stepi += 1
                                if stepi in (4, 6):
                                    next(filler, None)
                                if stepi % 4 == 0:
                                    next(pre, None)
                        for _ in filler:
                            pass
                        DSP(lambda e, ot=ot, h=h: e.dma_start(out=mix_d[s, h * 128:(h + 1) * 128, :], in_=ot[:]), [otk], ["mix_d"])
                    for _ in pre:
                        pass
                    S.emit()


            sR.close()
            if "D" in phases:
                gdn_phase(s)

            if "O" in phases:
                with contextlib.ExitStack() as so:
                    mixT = sbt(so, "mixT", [128, KT, L], BF16)
                    wo = Ring([sbt(so, "wo%d" % i, [128, KT, 512], BF16) for i in range(2)], "wo")
                    xc = Ring([sbt(so, "xc%d" % i, [128, 512], F32) for i in range(3)], "xc")
                    hc = Ring([sbt(so, "hc%d" % i, [128, 512], F32) for i in range(3)], "hc")
                    junk = sbt(so, "o_junk", [128, 512], BF16)
                    po_ = Ring([pst(so, "po%d" % i, [128, 512], F32) for i in range(4)], "po")
                    for k in range(KT):
                        (DSP if k % 2 == 0 else DPL)(lambda e, k=k: e.dma_start(out=mixT[:, k, :], in_=mix_d[s, k * 128:(k + 1) * 128, :]),
                            ["mix_d"], ["mixT%d" % k])
                    for cg in range(4):
                        wot, wok = wo.next()
                        DPL(lambda e, wot=wot, cg=cg: e.dma_start(
                            out=wot[:], in_=w_out[:, cg * 512:(cg + 1) * 512].rearrange("(k p) c -> p k c", p=128)), [], [wok])
                        for t in range(TT):
                            xct, xck = xc.next()
                            hct, hck = hc.next()
                            pt, ptk = po_.next()
                            DSP(lambda e, xct=xct, t=t, cg=cg: e.dma_start(
                                out=xct[:], in_=x[s, t * 128:(t + 1) * 128, cg * 512:(cg + 1) * 512]), [], [xck])
                            for k in range(KT):
                                PE(lambda e, pt=pt, k=k, t=t, wot=wot: e.matmul(
                                    pt[:], mixT[:, k, t * 128:(t + 1) * 128], wot[:, k, :], start=(k == 0), stop=(k == KT - 1)),
                                   ["mixT%d" % k, wok], [ptk])
                            DVE(lambda e, hct=hct, pt=pt, xct=xct: e.tensor_tensor(hct[:], pt[:], xct[:], ALU.add),
                                [ptk, xck], [hck])
                            ACT(lambda e, hct=hct, t=t, cg=cg: e.activation(junk[:], hct[:], AF.Square,
                                                                            accum_out=ssp[:, t, cg:cg + 1]),
                                [hck], ["o_junk", "ssp"])
                            DSP(lambda e, hct=hct, t=t, cg=cg: e.dma_start(
                                out=h_d[s, t * 128:(t + 1) * 128, cg * 512:(cg + 1) * 512], in_=hct[:]),
                                [hck], ["srcrows%d" % t])
                    DVE(lambda e: e.reduce_sum(ss_all[:], ssp[:], AX.X), ["ssp"], ["ss_all"])
                    ACT(lambda e: e.activation(rstd_all[:], ss_all[:], AF.Ln, bias=epsc[:, 0:1], scale=1.0 / D), ["ss_all", "epsc"], ["rstd_all0"])
                    ACT(lambda e: e.activation(rstd_all[:], rstd_all[:], AF.Exp, scale=-0.5), ["rstd_all0"],
                        ["rstd%d" % t for t in range(TT)])
                    S.emit()
            sR = contextlib.ExitStack()
            R1 = sbt(sR, "R1b", [128, KT, L], BF16)
            if "O" in phases:
                with contextlib.ExitStack() as so2:
                    norm_transpose(so2, lambda t: h_d[s, t * 128:(t + 1) * 128, :], nw_ffn, True)
                    S.emit()

            if "F" in phases:
                with contextlib.ExitStack() as sf:
                    actb = sbt(sf, "actb", [128, FC, 512], BF16)
                    wg = Ring([sbt(sf, "wg%d" % i, [128, KT, 256], BF16) for i in range(2)], "wg")
                    wu = Ring([sbt(sf, "wu%d" % i, [128, KT, 256], BF16) for i in range(2)], "wu")
                    wd = Ring([sbt(sf, "wd%d" % i, [128, 4, 512], BF16) for i in range(2)], "wd")
                    sgt = Ring([sbt(sf, "sg%d" % i, [128, 512], F32) for i in range(2)], "sg")
                    hcr = Ring([sbt(sf, "fh%d" % i, [128, 512], F32) for i in range(4)], "fh")
                    ycr = Ring([sbt(sf, "fy%d" % i, [128, 512], F32) for i in range(3)], "fy")
                    junk = sbt(sf, "f_junk", [128, 512], BF16)
                    pg = Ring([pst(sf, "pg%d" % i, [128, 512], F32) for i in range(2)], "pg")
                    pu = Ring([pst(sf, "pu%d" % i, [128, 512], F32) for i in range(2)], "pu")
                    pd = [pst(sf, "pd%d" % i, [128, 512], F32) for i in range(4)]
                    for blk in range(4):
                        c0 = blk * 512
                        for f2 in range(FC // 2):
                            wgt, wgk = wg.next()
                            wut, wuk = wu.next()
                            pc_ = 0 if f2 < 16 else 1
                            DSP(lambda e, wgt=wgt, f2=f2: e.dma_start(out=wgt[:], in_=wg_s[f2]),
                                ["wgs_%d_%d" % (k, pc_) for k in range(KT)], [wgk])
                            DSP(lambda e, wut=wut, f2=f2: e.dma_start(out=wut[:], in_=wu_s[f2]),
                                ["wus_%d_%d" % (k, pc_) for k in range(KT)], [wuk])
                            for fl in range(2):
                                fc = f2 * 2 + fl
                                pgt, pgk = pg.next()
                                put, puk = pu.next()
                                sg, sgk = sgt.next()
                                for k in range(KT):
                                    PE(lambda e, pgt=pgt, k=k, fl=fl, wgt=wgt, c0=c0: e.matmul(
                                        pgt[:], wgt[:, k, fl * 128:(fl + 1) * 128], R1[:, k, c0:c0 + 512],
                                        start=(k == 0), stop=(k == KT - 1)), R1all + [wgk], [pgk])
                                for k in range(KT):
                                    PE(lambda e, put=put, k=k, fl=fl, wut=wut, c0=c0: e.matmul(
                                        put[:], wut[:, k, fl * 128:(fl + 1) * 128], R1[:, k, c0:c0 + 512],
                                        start=(k == 0), stop=(k == KT - 1)), R1all + [wuk], [puk])
                                ACT(lambda e, sg=sg, pgt=pgt: e.activation(sg[:], pgt[:], AF.Silu), [pgk], [sgk])
                                DVE(lambda e, sg=sg, put=put, fc=fc: e.tensor_tensor(actb[:, fc, :], sg[:], put[:], ALU.mult),
                                    [sgk, puk], ["actb"])
                        for cg in range(4):
                            hcts = []
                            for tt in range(4):
                                t = blk * 4 + tt
                                hct, hck = hcr.next()
                                DPL(lambda e, hct=hct, t=t, cg=cg: e.dma_start(
                                    out=hct[:], in_=h_d[s, t * 128:(t + 1) * 128, cg * 512:(cg + 1) * 512]),
                                    ["srcrows%d" % t], [hck])
                                hcts.append((hct, hck))
                            for f4 in range(FC // 4):
                                wdt, wdk = wd.next()
                                DSP(lambda e, wdt=wdt, f4=f4, cg=cg: e.dma_start(out=wdt[:], in_=wd_s[f4, cg]),
                                    ["wds_%d" % (f4 * 4 + a_) for a_ in range(4)], [wdk])
                                for fl in range(4):
                                    fc = f4 * 4 + fl
                                    for tt in range(4):
                                        PE(lambda e, tt=tt, fc=fc, fl=fl, wdt=wdt: e.matmul(
                                            pd[tt][:], actb[:, fc, tt * 128:(tt + 1) * 128], wdt[:, fl, :],
                                            start=(fc == 0), stop=(fc == FC - 1)), ["actb", wdk], ["pd%d" % tt])
                            for tt in range(4):
                                t = blk * 4 + tt
                                hct, hck = hcts[tt]
                                yct, yck = ycr.next()
                                DVE(lambda e, yct=yct, tt=tt, hct=hct: e.tensor_tensor(yct[:], pd[tt][:], hct[:], ALU.add),
                                    ["pd%d" % tt, hck], [yck])
                                ACT(lambda e, yct=yct, t=t, cg=cg: e.activation(junk[:], yct[:], AF.Square,
                                                                                accum_out=ssp[:, t, cg:cg + 1]),
                                    [yck], ["f_junk", "ssp"])
                                DPL(lambda e, yct=yct, t=t, cg=cg: e.dma_start(
                                    out=y[s, t * 128:(t + 1) * 128, cg * 512:(cg + 1) * 512], in_=yct[:]), [yck], ["yrows%d" % t])
                    DVE(lambda e: e.reduce_sum(ss_all[:], ssp[:], AX.X), ["ssp"], ["ss_all"])
                    ACT(lambda e: e.activation(rstd_all[:], ss_all[:], AF.Ln, bias=epsc[:, 0:1], scale=1.0 / D), ["ss_all", "epsc"], ["rstd_all0"])
                    ACT(lambda e: e.activation(rstd_all[:], rstd_all[:], AF.Exp, scale=-0.5), ["rstd_all0"], ["rstd_allF"])
                    S.emit()
                with contextlib.ExitStack() as sf2:
                    fw = sbt(sf2, "fw", [128, D], F32)
                    yt = Ring([sbt(sf2, "yt%d" % i, [128, D], F32) for i in range(4)], "yt")
                    DSP(lambda e: e.dma_start(out=fw[:], in_=nw_fin_d.partition_broadcast(128)), [], ["fw"])
                    for t in range(TT):
                        ytt, ytk = yt.next()
                        DSP(lambda e, ytt=ytt, t=t: e.dma_start(out=ytt[:], in_=y[s, t * 128:(t + 1) * 128, :]),
                            ["yrows%d" % t], [ytk])
                        DVE(lambda e, ytt=ytt, t=t: e.scalar_tensor_tensor(ytt[:], ytt[:], rstd_all[:, t:t + 1], fw[:],
                                                                           ALU.mult, ALU.mult),
                            [ytk, "rstd_allF", "fw"], [ytk])
                        DPL(lambda e, ytt=ytt, t=t: e.dma_start(out=y[s, t * 128:(t + 1) * 128, :], in_=ytt[:]),
                            [ytk], ["yrows%d" % t])
                    S.emit()
            sR.close()
        S.emit(final=True)
    return nc


def host_consts(inp):
    f32 = np.float32
    c = {}
    c["consts"] = make_consts()
    c["nw_mix"] = np.ascontiguousarray(np.asarray(inp["norm_mix_w"], f32)[0].reshape(KT, 128).T)
    c["nw_ffn"] = np.ascontiguousarray(np.asarray(inp["norm_ffn_w"], f32)[0].reshape(KT, 128).T)
    c["nw_fin"] = np.ascontiguousarray(np.asarray(inp["final_norm_w"], f32).reshape(1, D))
    c["rpb"] = np.ascontiguousarray(np.asarray(inp["na_rpb"], f32)[0].reshape(1, -1))
    cw = np.asarray(inp["dn_conv_w"], f32)[0]
    c["convw"] = np.ascontiguousarray(cw.T.reshape(24, 128, 5).transpose(1, 0, 2).reshape(128, 120))
    c["alog"] = np.ascontiguousarray(np.asarray(inp["dn_A_log"], f32)[0].reshape(1, 16))
    c["dtb"] = np.ascontiguousarray(np.asarray(inp["dn_dt_bias"], f32)[0].reshape(1, 16))
    c["dnw"] = np.ascontiguousarray(np.asarray(inp["dn_norm_w"], f32)[0].reshape(1, 128))
    c["w_in"] = np.ascontiguousarray(np.asarray(inp["w_in"], f32)[0])
    c["w_out"] = np.ascontiguousarray(np.asarray(inp["w_out"], f32)[0])
    c["w_gate"] = np.ascontiguousarray(np.asarray(inp["w_gate"], f32)[0])
    c["w_up"] = np.ascontiguousarray(np.asarray(inp["w_up"], f32)[0])
    c["w_down"] = np.ascontiguousarray(np.asarray(inp["w_down"], f32)[0])
    return c


def kernel(**inputs):
    xp = np.asarray(inputs["x_prompt"], np.float32)
    xs = np.asarray(inputs["x_sample"], np.float32)
    shared = host_consts(inputs)
    nc = build(nslot=2)
    in_maps = []
    for c in range(8):
        m = dict(shared)
        m["x"] = np.ascontiguousarray(np.stack([xp[c], xs[c % 4]], axis=0))
        in_maps.append(m)
    res = run_bass_kernel_spmd(nc, in_maps, core_ids=list(range(8)))
    yp = np.stack([np.asarray(res.results[c]["y"])[0] for c in range(8)], axis=0)
    ys = np.stack([np.asarray(res.results[c]["y"])[1] for c in range(4)], axis=0)
    return (yp.astype(np.float32), ys.astype(np.float32))
```

```python
import contextlib
import numpy as np
import concourse.bass as bass
import concourse.mybir as mybir
from concourse.bass_utils import run_bass_kernel_spmd

F32 = mybir.dt.float32
BF16 = mybir.dt.bfloat16
AF = mybir.ActivationFunctionType
ALU = mybir.AluOpType
AX = mybir.AxisListType

D = 2048
L = 2048
KT = 16
TT = 16
HD = 128
IN_COLS = 7200
OFF_DN = 3072
OFF_Z = 6144
OFF_B = 7168
OFF_A = 7184
DFF = 5632
FC = DFF // 128
NEG = -30000.0
EPS = 1e-6


class Op:
    __slots__ = ("eng", "fn", "deps", "dma", "n", "sem", "semval", "prewait")


class Sched:
    CE = ("pe", "act", "dve", "pool")

    def __init__(self, nc, stack, n_dma_sems=8):
        self.nc = nc
        self.ops = []
        self.last_w = {}
        self.readers = {}
        self.ndma = n_dma_sems
        self.csem = {e: stack.enter_context(nc.semaphore("cs_" + e)) for e in self.CE}
        self.dsem = {
            q: [stack.enter_context(nc.semaphore("ds_%s%d" % (q, i))) for i in range(n_dma_sems)]
            for q in ("sp", "act", "pool")
        }
        self.dcount = {"sp": 0, "act": 0, "pool": 0}
        self.cnt = {e: 0 for e in self.CE}
        self.final_dma = {}
        self.nops = 0

    def add(self, eng, fn, reads=(), writes=(), dma=False):
        op = Op()
        op.eng, op.fn, op.dma = eng, fn, dma
        op.prewait = None
        deps = []
        for t in reads:
            w = self.last_w.get(t)
            if w is not None:
                deps.append(w)
        for t in writes:
            w = self.last_w.get(t)
            if w is not None:
                deps.append(w)
            r = self.readers.get(t)
            if r:
                deps.extend(r.values())
        for t in writes:
            self.last_w[t] = op
            self.readers[t] = {}
        for t in reads:
            d = self.readers.setdefault(t, {})
            if dma:
                d[("dma", id(op))] = op
            else:
                d[eng] = op
        if dma:
            j = self.dcount[eng]
            self.dcount[eng] += 1
            op.sem = self.dsem[eng][j % self.ndma]
            op.semval = 16 * (j // self.ndma + 1)
            if j >= self.ndma:
                op.prewait = (op.sem, 16 * (j // self.ndma))
        else:
            self.cnt[eng] += 1
            op.n = self.cnt[eng]
        op.deps = [d for d in deps if d is not op]
        self.ops.append(op)
        self.nops += 1
        return op

    def emit(self, final=False):
        per = {e: [] for e in ("pe", "act", "dve", "pool", "sp")}
        for op in self.ops:
            per[op.eng].append(op)
            if op.dma:
                self.final_dma[id(op.sem)] = (op.sem, op.semval)
        final_dma = self.final_dma

        def run(engname, eng):
            waited = {}
            for op in per[engname]:
                if op.prewait is not None:
                    s, v = op.prewait
                    if waited.get(id(s), 0) < v:
                        eng.wait_ge(s, v)
                        waited[id(s)] = v
                for d in op.deps:
                    if d.dma:
                        s, v = d.sem, d.semval
                    else:
                        if d.eng == engname and d.eng == "pe" and not op.dma:
                            continue
                        s, v = self.csem[d.eng], d.n
                    if waited.get(id(s), 0) < v:
                        eng.wait_ge(s, v)
                        waited[id(s)] = v
                ins = op.fn(eng)
                if op.dma:
                    ins.then_inc(op.sem, 16)
                else:
                    ins.then_inc(self.csem[op.eng], 1)
            if engname == "sp":
                for s, v in final_dma.values():
                    if waited.get(id(s), 0) < v:
                        eng.wait_ge(s, v)

        with self.nc.Block() as block:
            @block.tensor
            def _(e):
                run("pe", e)

            @block.scalar
            def _(e):
                run("act", e)

            @block.vector
            def _(e):
                run("dve", e)

            @block.gpsimd
            def _(e):
                run("pool", e)

            @block.sync
            def _(e):
                run("sp", e)
        self.ops = []


class Ring:
    def __init__(self, tiles, name):
        self.tiles, self.name, self.i = tiles, name, 0

    def next(self):
        k = self.i % len(self.tiles)
        self.i += 1
        return self.tiles[k], "%s%d" % (self.name, k)


def make_consts():
    i = np.arange(128)
    t = i[:, None]
    j = i[None, :]
    c = np.zeros((128, 11, 128), np.float32)
    c[:, 0] = (t == j)
    c[:, 1] = (t <= j)
    c[:, 2] = (t >= j)
    c[:, 3] = (t > j)
    c[:, 4] = (t < j)
    c[:, 5] = np.where(t >= j, 0.0, NEG)
    c[:, 6] = np.where(t <= j, 0.0, NEG)
    c[:, 7] = (t != j)
    c[:, 8] = 1.0
    c[:, 9] = -1.0
    c[:, 10] = NEG
    return c.reshape(128, 11 * 128)


def build(nslot=2, dbg=False, phases="AGNDOF"):
    nc = bass.Bass("TRN2", target_bir_lowering=False)
    dt_in = lambda name, shape, dt=F32: nc.dram_tensor(name, list(shape), dt, kind="ExternalInput").ap()
    x = dt_in("x", [nslot, L, D])
    w_in = dt_in("w_in", [D, IN_COLS])
    w_out = dt_in("w_out", [D, D])
    w_gate = dt_in("w_gate", [D, DFF])
    w_up = dt_in("w_up", [D, DFF])
    w_down = dt_in("w_down", [DFF, D])
    consts_d = dt_in("consts", [128, 11 * 128])
    nw_mix_d = dt_in("nw_mix", [128, KT])
    nw_ffn_d = dt_in("nw_ffn", [128, KT])
    nw_fin_d = dt_in("nw_fin", [1, D])
    rpb_d = dt_in("rpb", [1, 8 * 15 * 31])
    convw_d = dt_in("convw", [128, 24 * 5])
    alog_d = dt_in("alog", [1, 16])
    dtb_d = dt_in("dtb", [1, 16])
    dnw_d = dt_in("dnw", [1, 128])
    y = nc.dram_tensor("y", [nslot, L, D], F32, kind="ExternalOutput").ap()
    mixk = "ExternalOutput" if dbg else "Internal"
    mix_d = nc.dram_tensor("mix_d", [nslot, D, L], BF16, kind=mixk).ap()
    h_d = nc.dram_tensor("h_d", [nslot, L, D], F32, kind=mixk).ap()
    wg_s = nc.dram_tensor("wg_s", [FC // 2, 128, KT, 256], BF16, kind="Internal").ap()
    wu_s = nc.dram_tensor("wu_s", [FC // 2, 128, KT, 256], BF16, kind="Internal").ap()
    wd_s = nc.dram_tensor("wd_s", [FC // 4, 4, 128, 4, 512], BF16, kind="Internal").ap()
    xn_s = nc.dram_tensor("xn_s", [KT, 128, L], BF16, kind="Internal").ap()
    tb_h = nc.dram_tensor("tb_d", [64, 8, 15, 64], F32, kind="Internal")
    tb_d = tb_h.ap()

    with contextlib.ExitStack() as st:
        S = Sched(nc, st)
        uid = [0]

        def sbt(stack, name, shape, dt):
            uid[0] += 1
            return stack.enter_context(nc.sbuf_tensor("%s_u%d" % (name, uid[0]), list(shape), dt))

        def pst(stack, name, shape, dt):
            uid[0] += 1
            return stack.enter_context(nc.psum_tensor("%s_u%d" % (name, uid[0]), list(shape), dt))
        PE = lambda fn, r, w: S.add("pe", fn, r, w)
        ACT = lambda fn, r, w: S.add("act", fn, r, w)
        DVE = lambda fn, r, w: S.add("dve", fn, r, w)
        POOL = lambda fn, r, w: S.add("pool", fn, r, w)
        DSP = lambda fn, r, w: S.add("sp", fn, r, w, dma=True)
        DPL = lambda fn, r, w: S.add("pool", fn, r, w, dma=True)

        R1 = None
        cst = sbt(st, "cst", [128, 11, 128], F32)
        ident_bf = sbt(st, "ident_bf", [128, 128], BF16)
        ones_bf = sbt(st, "ones_bf", [128, 128], BF16)
        nw_mix = sbt(st, "nw_mix", [128, KT], F32)
        nw_ffn = sbt(st, "nw_ffn", [128, KT], F32)
        convw = sbt(st, "convw", [128, 24, 5], F32)
        alog = sbt(st, "alog", [128, 16], F32)
        negA = sbt(st, "negA", [128, 16], F32)
        dtb = sbt(st, "dtb", [128, 16], F32)
        dnw = sbt(st, "dnw", [128, 128], F32)
        beta_all = sbt(st, "beta_all", [128, TT, 16], F32)
        g_all = sbt(st, "g_all", [128, TT, 16], F32)
        ssp = sbt(st, "ssp", [128, TT, 4], F32)
        ss_all = sbt(st, "ss_all", [128, TT], F32)
        rstd_all = sbt(st, "rstd_all", [128, TT], F32)
        epsc = sbt(st, "epsc", [128, 1], F32)
        ident_f = cst[:, 0, :]
        ones_f = cst[:, 8, :]
        negones_f = cst[:, 9, :]

        with contextlib.ExitStack() as s0:
            rpb_sb = sbt(s0, "rpb_sb", [64, 8, 15, 31], F32)
            negt = sbt(s0, "negt", [64, 960], F32)
            DSP(lambda e: e.dma_start(out=cst[:].rearrange("p a b -> p (a b)"), in_=consts_d), [], ["cst"])
            DSP(lambda e: e.dma_start(out=nw_mix[:], in_=nw_mix_d), [], ["nw_mix"])
            DSP(lambda e: e.dma_start(out=nw_ffn[:], in_=nw_ffn_d), [], ["nw_ffn"])
            DSP(lambda e: e.dma_start(out=convw[:].rearrange("p a b -> p (a b)"), in_=convw_d), [], ["convw"])
            DSP(lambda e: e.dma_start(out=alog[:], in_=alog_d.partition_broadcast(128)), [], ["alog"])
            DSP(lambda e: e.dma_start(out=dtb[:], in_=dtb_d.partition_broadcast(128)), [], ["dtb"])
            DSP(lambda e: e.dma_start(out=dnw[:], in_=dnw_d.partition_broadcast(128)), [], ["dnw"])
            DSP(lambda e: e.dma_start(out=rpb_sb[:].rearrange("p a b c -> p (a b c)"),
                                      in_=rpb_d.partition_broadcast(64)), [], ["rpb_sb"])
            POOL(lambda e: e.memset(epsc[:], EPS), [], ["epsc"])
            DVE(lambda e: e.tensor_copy(ident_bf[:], cst[:, 0, :]), ["cst"], ["ident_bf"])
            DVE(lambda e: e.tensor_copy(ones_bf[:], cst[:, 8, :]), ["cst"], ["ones_bf"])
            ACT(lambda e: e.activation(negA[:], alog[:], AF.Exp), ["alog"], ["negA0"])
            DVE(lambda e: e.tensor_scalar(negA[:], negA[:], -1.0, None, ALU.mult), ["negA0"], ["negA"])
            if "N" in phases or "n" in phases:
                POOL(lambda e: e.memset(negt[:], NEG), [], ["negt"])
                for h in range(8):
                    DSP(lambda e, h=h: e.dma_start(out=tb_d[:, h, :, :].rearrange("p a b -> p (a b)"), in_=negt[:]),
                        ["negt"], ["tb%d" % h])
                for h in range(8):
                    dst = bass.AP(tb_h, 8 * 7680 + h * 960, [[7681, 49], [64, 15], [1, 16]])
                    DSP(lambda e, h=h, dst=dst: e.dma_start(out=dst, in_=rpb_sb[8:57, h, :, 7:23]),
                        ["rpb_sb", "tb%d" % h], ["tb%d" % h])
                for c in list(range(0, 8)) + list(range(57, 64)):
                    cs = min(max(c - 8, 0), 48)
                    off = cs - c + 15
                    for h in range(8):
                        DSP(lambda e, c=c, cs=cs, off=off, h=h: e.dma_start(
                            out=tb_d[c:c + 1, h, :, cs:cs + 16], in_=rpb_sb[c:c + 1, h, :, off:off + 16]),
                            ["rpb_sb", "tb%d" % h], ["tb%d" % h])
            S.emit()

        def norm_transpose(stk, src_of_tile, nw, stats_from_ssall):
            xts = Ring([sbt(stk, "nt_x%d" % i, [128, D], F32) for i in range(4)], "nt_x")
            xss = Ring([sbt(stk, "nt_s%d" % i, [128, D], BF16) for i in range(3)], "nt_s")
            junk = sbt(stk, "nt_junk", [128, D], BF16)
            tps = Ring([pst(stk, "nt_tp%d" % i, [128, 8, 128], BF16) for i in range(4)], "nt_tp")
            for t in range(TT):
                xt, xtk = xts.next()
                xs, xsk = xss.next()
                (DSP if t % 2 == 0 else DPL)(lambda e, xt=xt, t=t: e.dma_start(out=xt[:], in_=src_of_tile(t)), ["srcrows%d" % t], [xtk])
                if not stats_from_ssall:
                    ACT(lambda e, xt=xt, t=t: e.activation(junk[:], xt[:], AF.Square, accum_out=ss_all[:, t:t + 1]),
                        [xtk], ["junk", "ss%d" % t])
                    ACT(lambda e, t=t: e.activation(rstd_all[:, t:t + 1], ss_all[:, t:t + 1], AF.Ln, bias=epsc[:, 0:1],
                                                    scale=1.0 / D), ["ss%d" % t, "epsc"], ["rs0%d" % t])
                    ACT(lambda e, t=t: e.activation(rstd_all[:, t:t + 1], rstd_all[:, t:t + 1], AF.Exp, scale=-0.5),
                        ["rs0%d" % t], ["rstd%d" % t])
                ACT(lambda e, xt=xt, xs=xs, t=t: e.activation(xs[:], xt[:], AF.Copy, scale=rstd_all[:, t:t + 1]),
                    [xtk, "rstd%d" % t], [xsk])
                for half in range(2):
                    tp, tpk = tps.next()
                    for j in range(8):
                        k = half * 8 + j
                        PE(lambda e, tp=tp, xs=xs, j=j, k=k: e.transpose(tp[:, j, :], xs[:, k * 128:(k + 1) * 128],
                                                                         ident_bf[:]),
                           [xsk, "ident_bf"], [tpk])
                    DVE(lambda e, tp=tp, half=half, t=t: e.tensor_tensor(
                        R1[:, half * 8:(half + 1) * 8, t * 128:(t + 1) * 128], tp[:],
                        nw[:, half * 8:(half + 1) * 8].unsqueeze(2).broadcast_to([128, 8, 128]), ALU.mult),
                        [tpk, "nw_mix", "nw_ffn"], ["R1_%d" % t])

        R1all = ["R1_%d" % t for t in range(TT)]

        def gdn_phase(s):
            import os
            CUT = int(os.environ.get("GDN_CUT", "99"))
            NPAIR = int(os.environ.get("GDN_NPAIR", "4"))
            NIT = int(os.environ.get("GDN_NIT", "16"))
            nonlocal R1
            with contextlib.ExitStack() as sd:
                qT2 = sbt(sd, "g_qT2", [128, 2, L], BF16)
                kT2 = sbt(sd, "g_kT2", [128, 2, L], BF16)
                vT2 = sbt(sd, "g_vT2", [128, 2, L], BF16)
                zs = sbt(sd, "g_zs", [128, TT, 256], BF16)
                ostore = sbt(sd, "g_ostore", [128, 8, 4, 128], F32)
                S4 = [sbt(sd, "g_S%d" % i, [128, 4, 128], F32) for i in range(2)]
                for hp in range(NPAIR):
                    with contextlib.ExitStack() as sp_:
                        wch = Ring([sbt(sp_, "g_wch%d" % i, [128, KT, 128], BF16) for i in range(3)], "g_wch")
                        wz = sbt(sp_, "g_wz", [128, KT, 256], BF16)
                        cpads = Ring([sbt(sp_, "g_cpad%d" % i, [128, L + 4], F32) for i in range(2)], "g_cpadR")
                        posts = Ring([sbt(sp_, "g_post%d" % i, [128, L], F32) for i in range(2)], "g_postR")
                        sq = sbt(sp_, "g_sq", [128, L], BF16)
                        rnb = Ring([sbt(sp_, "g_rnb%d" % i, [128, 512], F32) for i in range(2)], "g_rnb")
                        pcb = Ring([pst(sp_, "g_pc%d" % i, [128, 512], F32) for i in range(4)], "g_pc")
                        pzb = Ring([pst(sp_, "g_pz%d" % i, [128, 2, 256], F32) for i in range(2)], "g_pz")
                        R1 = sbt(sp_, "R1s", [128, KT, L], BF16)
                        for k in range(KT):
                            (DSP if k % 2 == 0 else DPL)(lambda e, k=k: e.dma_start(out=R1[:, k, :], in_=xn_s[k]), ["xn_s%d" % k], ["R1re_%d" % k])
                        R1deps = ["R1re_%d" % k for k in range(KT)]
                        for ci in range(2):
                            POOL(lambda e, ci=ci: e.memset(cpads.tiles[ci][:, 0:2], 0.0), [], ["g_cpadR%d" % ci])
                            POOL(lambda e, ci=ci: e.memset(cpads.tiles[ci][:, L + 2:L + 4], 0.0), ["g_cpadR%d" % ci], ["g_cpadR%d" % ci])
                        DPL(lambda e, hp=hp: e.dma_start(
                            out=wz[:], in_=w_in[:, OFF_Z + hp * 256: OFF_Z + (hp + 1) * 256].rearrange("(k p) c -> p k c", p=128)),
                            [], ["g_wz"])
                        pend_l2 = []

                        def l2_tail(kind, hh, post, kpo):
                            dst = qT2 if kind == 0 else kT2
                            dk_ = "g_qT2" if kind == 0 else "g_kT2"
                            scl = float(HD ** -0.5) if kind == 0 else 1.0
                            for tg in range(4):
                                pc, pck = pcb.next()
                                rn, rnk = rnb.next()
                                PE(lambda e, pc=pc, tg=tg: e.matmul(pc[:], ones_bf[:], sq[:, tg * 512:(tg + 1) * 512],
                                                                    start=True, stop=True), ["g_sq", "ones_bf"], [pck])
                                ACT(lambda e, pc=pc, rn=rn: e.activation(rn[:], pc[:], AF.Ln, bias=epsc[:, 0:1]), [pck, "epsc"], [rnk])
                                ACT(lambda e, rn=rn: e.activation(rn[:], rn[:], AF.Exp, scale=-0.5), [rnk], [rnk])
                                DVE(lambda e, rn=rn, tg=tg, dst=dst, hh=hh, scl=scl, post=post: e.scalar_tensor_tensor(
                                    dst[:, hh, tg * 512:(tg + 1) * 512], post[:, tg * 512:(tg + 1) * 512], scl, rn[:],
                                    ALU.mult, ALU.mult), [kpo, rnk], [dk_])

                        for kind in range(3):
                            for hh in range(2):
                                h = 2 * hp + hh
                                col = OFF_DN + kind * 1024 + h * 128
                                cidx = kind * 8 + h
                                wt, wk_ = wch.next()
                                cpad, kcp = cpads.next()
                                post, kpo = posts.next()
                                DPL(lambda e, wt=wt, col=col: e.dma_start(
                                    out=wt[:], in_=w_in[:, col:col + 128].rearrange("(k p) c -> p k c", p=128)), [], [wk_])
                                for tg in range(4):
                                    pc, pck = pcb.next()
                                    for k in range(KT):
                                        PE(lambda e, pc=pc, k=k, tg=tg, wt=wt: e.matmul(
                                            pc[:], wt[:, k, :], R1[:, k, tg * 512:(tg + 1) * 512], start=(k == 0), stop=(k == KT - 1)),
                                           R1deps + [wk_], [pck])
                                    ACT(lambda e, pc=pc, tg=tg, cpad=cpad: e.copy(cpad[:, 2 + tg * 512: 2 + (tg + 1) * 512], pc[:]),
                                        [pck], [kcp])
                                if pend_l2:
                                    l2_tail(*pend_l2.pop(0))
                                DVE(lambda e, cidx=cidx, post=post, cpad=cpad: e.tensor_scalar(post[:], cpad[:, 0:L], convw[:, cidx, 0:1], None, ALU.mult),
                                    [kcp, "convw"], [kpo])
                                for j in range(1, 5):
                                    DVE(lambda e, cidx=cidx, j=j, post=post, cpad=cpad: e.scalar_tensor_tensor(
                                        post[:], cpad[:, j:j + L], convw[:, cidx, j:j + 1], post[:], ALU.mult, ALU.add),
                                        [kcp, "convw", kpo], [kpo])
                                if kind == 2:
                                    ACT(lambda e, hh=hh, post=post: e.activation(vT2[:, hh, :], post[:], AF.Silu), [kpo], ["g_vT2"])
                                else:
                                    ACT(lambda e, post=post: e.activation(post[:], post[:], AF.Silu), [kpo], [kpo])
                                    ACT(lambda e, post=post: e.activation(sq[:], post[:], AF.Square), [kpo], ["g_sq"])
                                    pend_l2.append((kind, hh, post, kpo))
                        while pend_l2:
                            kind_, hh_, post, kpo = pend_l2.pop(0)
                            l2_tail(kind_, hh_, post, kpo)
                        if False:
                            if False:
                                if False:
                                    for tg in range(4):
                                        pc, pck = pcb.next()
                                        rn, rnk = rnb.next()
                                        PE(lambda e, pc=pc, tg=tg: e.matmul(pc[:], ones_bf[:], sq[:, tg * 512:(tg + 1) * 512],
                                                                            start=True, stop=True), ["g_sq", "ones_bf"], [pck])
                                        ACT(lambda e, pc=pc, rn=rn: e.activation(rn[:], pc[:], AF.Ln, bias=epsc[:, 0:1]), [pck, "epsc"], [rnk])
                                        ACT(lambda e, rn=rn: e.activation(rn[:], rn[:], AF.Exp, scale=-0.5), [rnk], [rnk])
                                        DVE(lambda e, rn=rn, tg=tg, dst=dst, hh=hh, scl=scl, post=post: e.scalar_tensor_tensor(
                                            dst[:, hh, tg * 512:(tg + 1) * 512], post[:, tg * 512:(tg + 1) * 512], scl, rn[:],
                                            ALU.mult, ALU.mult), [kpo, rnk], [dk_])
                        for t2 in range(TT // 2):
                            pz, pzk = pzb.next()
                            for tl in range(2):
                                t = t2 * 2 + tl
                                for k in range(KT):
                                    PE(lambda e, pz=pz, tl=tl, t=t, k=k: e.matmul(
                                        pz[:, tl, :], R1[:, k, t * 128:(t + 1) * 128], wz[:, k, :], start=(k == 0), stop=(k == KT - 1)),
                                       R1deps + ["g_wz"], [pzk])
                            ACT(lambda e, pz=pz, t2=t2: e.activation(zs[:, t2 * 2:(t2 + 1) * 2, :], pz[:], AF.Silu), [pzk], ["g_zs"])
                        S.emit()
                    with contextlib.ExitStack() as si:
                        sets = []
                        NSETS = int(os.environ.get("GDN_NSETS", "4"))
                        for q in range(NSETS):
                            B = {}
                            for bi_, nm in enumerate(("b1", "b2", "b3", "b4", "b5", "b6", "b7", "b8", "b9", "b10", "b11", "b12")):
                                B[nm] = sbt(si, "g_%s_%d" % (nm, q), [128, 4, 128], F32)
                            B["R4"] = sbt(si, "g_R4_%d" % q, [128, 4, 256], BF16)
                            B["UW4"] = sbt(si, "g_UW4_%d" % q, [128, 4, 256], BF16)
                            B["qkTb"] = sbt(si, "g_qkTb_%d" % q, [128, 4, 128], BF16)
                            B["kdb"] = sbt(si, "g_kdb_%d" % q, [128, 4, 128], BF16)
                            B["TTb"] = sbt(si, "g_TTb_%d" % q, [128, 4, 128], BF16)
                            B["yT4"] = sbt(si, "g_yT4_%d" % q, [128, 4, 128], BF16)
                            B["y4"] = sbt(si, "g_y4_%d" % q, [128, 4, 128], BF16)
                            B["gs"] = sbt(si, "g_gs_%d" % q, [128, 3, 4], F32)
                            B["bt"] = sbt(si, "g_bt_%d" % q, [128, 2, 4], F32)
                            B["st4"] = sbt(si, "g_st4_%d" % q, [128, 2, 4], F32)
                            B["Gc4"] = sbt(si, "g_Gc4_%d" % q, [128, 4], F32)
                            sets.append(B)
                        gb = Ring([pst(si, "g_b%d" % i, [128, 4, 128], F32) for i in range(6)], "g_b")
                        gbb = Ring([pst(si, "g_bb%d" % i, [128, 8, 128], BF16) for i in range(2)], "g_bb")
                        POOL(lambda e: e.memset(S4[0][:], 0.0), [], ["g_S0"])
                        NEG4c = sbt(si, "g_NEG4c", [128, 4, 128], F32)
                        for d_ in range(2):
                            POOL(lambda e, d_=d_: e.tensor_copy(NEG4c[:, 2 * d_:2 * d_ + 2, :],
                                                                cst[:, 5 + d_, :].unsqueeze(1).broadcast_to([128, 2, 128])),
                                 ["cst"], ["g_NEG4c"])
                        GM = [cst[:, 1, :], cst[:, 2, :]]
                        GD = [cst[:, 3, :], cst[:, 4, :]]
                        NG = [cst[:, 5, :], cst[:, 6, :]]

                        def iteration(n, q):
                            B = sets[q]
                            K_ = lambda nm: "g_%s_%d" % (nm, q)
                            Mg4, kd4 = B["b1"], B["kdb"]
                            D4, Wn4 = B["b2"], B["b2"]
                            E4, Se4 = B["b3"], B["b3"]
                            Ds4, N4 = B["b4"], B["b4"]
                            A4, os4 = B["b5"], B["b5"]
                            AT4, sq4 = B["b6"], B["b6"]
                            qk4, qt4 = B["b7"], B["b7"]
                            qkT4 = B["qkTb"]
                            Pb_, PTb_, Xa, Xb = B["b9"], B["b10"], B["b11"], B["b12"]
                            R4, UW4, y4, gs, bt, st4 = B["R4"], B["UW4"], B["y4"], B["gs"], B["bt"], B["st4"]
                            M4, Gc4, kgc = B["b9"], B["Gc4"], K_("Gc4")
                            kMg, kD, kE, kDs, kA, kAT, kqk, kqkT = K_("b1"), K_("b2"), K_("b3"), K_("b4"), K_("b5"), K_("b6"), K_("b7"), K_("b8")
                            kkd, kWn, kSe, kN, kos, ksq, kqt = K_("kdb"), kD, kE, kDs, kA, kAT, kqk
                            kqkT = K_("qkTb")
                            TTb, kTTb = B["TTb"], K_("TTb")
                            kR4, kUW, ky4, kgs, kbt, kst = K_("R4"), K_("UW4"), K_("y4"), K_("gs"), K_("bt"), K_("st4")
                            chunk = [n, 15 - n]
                            gc0 = [2 * hp, 8 + 2 * hp]
                            cs_ = lambda d: slice(chunk[d] * 128, (chunk[d] + 1) * 128)
                            b_, bk_ = gb.next()
                            gsp = b_[:, 0, 0:12].rearrange("p (a b) -> p a b", a=3)
                            for d in range(2):
                                for kind, msk in enumerate((GM[d], GD[d], ones_f)):
                                    PE(lambda e, gsp=gsp, kind=kind, d=d, msk=msk, c=chunk[d], g0=gc0[d]: e.matmul(
                                        gsp[:, kind, 2 * d:2 * d + 2], msk, g_all[:, c, g0:g0 + 2], start=True, stop=True),
                                       ["g_all", "cst"], [bk_])
                            ACT(lambda e, gsp=gsp: e.copy(Gc4[:], gsp[:, 0, :]), [bk_], [kgc])
                            ACT(lambda e, gsp=gsp: e.activation(gs[:], gsp, AF.Exp), [bk_], [kgs])
                            for d in range(2):
                                DVE(lambda e, d=d, c=chunk[d], g0=gc0[d]: e.tensor_copy(bt[:, 0, 2 * d:2 * d + 2], beta_all[:, c, g0:g0 + 2]),
                                    ["beta_all"], [kbt])
                            DVE(lambda e: e.tensor_tensor(bt[:, 1, :], bt[:, 0, :], gs[:, 0, :], ALU.mult), [kbt, kgs], [kbt])
                            for d in range(2):
                                DVE(lambda e, d=d, c=chunk[d], g0=gc0[d]: e.tensor_tensor(
                                    Mg4[:, 2 * d:2 * d + 2, :], GM[d].unsqueeze(1).broadcast_to([128, 2, 128]),
                                    g_all[:, c, g0:g0 + 2].unsqueeze(2).broadcast_to([128, 2, 128]), ALU.mult),
                                    ["g_all", "cst"], [kMg])
                            yield
                            POOL(lambda e: e.tensor_tensor(M4[:], NEG4c[:], Gc4[:].unsqueeze(2).broadcast_to([128, 4, 128]), ALU.add),
                                 [kgc, "g_NEG4c"], [K_("b9")])
                            b_, bk_ = gb.next()
                            for s_ in range(4):
                                PE(lambda e, b_=b_, s_=s_: e.matmul(b_[:, s_, :], ones_f, Mg4[:, s_, :], start=True, stop=True),
                                   [kMg, "cst"], [bk_])
                            ACT(lambda e, b_=b_: e.activation(E4[:], b_[:], AF.Exp), [bk_], [kE])
                            DVE(lambda e, b_=b_: e.scalar_tensor_tensor(D4[:], b_[:], -1.0, M4[:], ALU.mult, ALU.add),
                                [bk_, kE, K_("b9")], [kD])
                            ACT(lambda e: e.activation(D4[:], D4[:], AF.Exp), [kD], [kD])
                            bkk, bkkk = gb.next()
                            bqk, bqkk = gb.next()
                            for s_ in range(4):
                                d, hh = s_ // 2, s_ % 2
                                PE(lambda e, bkk=bkk, s_=s_, hh=hh, c=cs_(d): e.matmul(bkk[:, s_, :], kT2[:, hh, c], kT2[:, hh, c],
                                                                                  start=True, stop=True), ["g_kT2"], [bkkk])
                            for s_ in range(4):
                                d, hh = s_ // 2, s_ % 2
                                PE(lambda e, bqk=bqk, s_=s_, hh=hh, c=cs_(d): e.matmul(bqk[:, s_, :], qT2[:, hh, c], kT2[:, hh, c],
                                                                                  start=True, stop=True), ["g_kT2", "g_qT2"], [bqkk])
                            POOL(lambda e: e.tensor_tensor(Ds4[:], D4[:], cst[:, 7, :].unsqueeze(1).broadcast_to([128, 4, 128]), ALU.mult),
                                 [kD, "cst"], [kDs])
                            POOL(lambda e: e.tensor_tensor(Ds4[:], Ds4[:], bt[:, 0, :].unsqueeze(2).broadcast_to([128, 4, 128]), ALU.mult),
                                 [kDs, kbt], [kDs])
                            DVE(lambda e, bkk=bkk: e.tensor_tensor(A4[:], bkk[:], Ds4[:], ALU.mult), [bkkk, kDs], [kA])
                            DVE(lambda e, bqk=bqk: e.tensor_tensor(qk4[:], bqk[:], D4[:], ALU.mult), [bqkk, kD], [kqk])
                            yield
                            b_, bk_ = gb.next()
                            for s_ in range(4):
                                PE(lambda e, b_=b_, s_=s_: e.transpose(b_[:, s_, :], A4[:, s_, :], ident_f), [kA, "cst"], [bk_])
                            ACT(lambda e, b_=b_: e.copy(AT4[:], b_[:]), [bk_], [kAT])
                            DVE(lambda e: e.tensor_tensor(Xa[:], ident_f.unsqueeze(1).broadcast_to([128, 4, 128]), AT4[:], ALU.subtract),
                                [kAT, "cst"], [K_("b11")])
                            b_, bk_ = gb.next()
                            for s_ in range(4):
                                PE(lambda e, b_=b_, s_=s_: e.transpose(b_[:, s_, :], qk4[:, s_, :], ident_f), [kqk, "cst"], [bk_])
                            ACT(lambda e, b_=b_: e.copy(qkT4[:], b_[:]), [bk_], [kqkT])
                            yield
                            Pp, Ppk, PTp, PTpk = A4, kA, AT4, kAT
                            Pn_, Pnk, PTn, PTnk = Pb_, K_("b9"), PTb_, K_("b10")
                            Xp, Xpk, Xn, Xnk = Xa, K_("b11"), Xb, K_("b12")
                            for m in range(1, 7):
                                b_, bk_ = gb.next()
                                for s_ in range(4):
                                    PE(lambda e, b_=b_, s_=s_, PTp=PTp, Pp=Pp: e.matmul(b_[:, s_, :], PTp[:, s_, :], Pp[:, s_, :],
                                                                                     start=True, stop=True), [Ppk, PTpk], [bk_])
                                ACT(lambda e, b_=b_, Pn_=Pn_: e.copy(Pn_[:], b_[:]), [bk_], [Pnk])
                                if m < 6:
                                    yield
                                    b2, b2k = gb.next()
                                    for s_ in range(4):
                                        PE(lambda e, b2=b2, s_=s_, Pn_=Pn_: e.transpose(b2[:, s_, :], Pn_[:, s_, :], ident_f), [Pnk, "cst"], [b2k])
                                    ACT(lambda e, b2=b2, PTn=PTn: e.copy(PTn[:], b2[:]), [b2k], [PTnk])
                                yield
                                b3, b3k = gb.next()
                                for s_ in range(4):
                                    PE(lambda e, b3=b3, s_=s_, Pn_=Pn_, Xp=Xp: e.matmul(b3[:, s_, :], Pn_[:, s_, :], Xp[:, s_, :],
                                                                                     start=True, stop=True), [Pnk, Xpk], [b3k])
                                DVE(lambda e, b3=b3, Xn=Xn, Xp=Xp: e.tensor_tensor(Xn[:], Xp[:], b3[:], ALU.add), [b3k, Xpk], [Xnk])
                                yield
                                Pp, Ppk, Pn_, Pnk = Pn_, Pnk, Pp, Ppk
                                PTp, PTpk, PTn, PTnk = PTn, PTnk, PTp, PTpk
                                Xp, Xpk, Xn, Xnk = Xn, Xnk, Xp, Xpk
                            ACT(lambda e, Xp=Xp: e.copy(TTb[:], Xp[:]), [Xpk], [kTTb])
                            TT_, TTk = TTb, kTTb
                            bb_, bbk = gbb.next()
                            for d in range(2):
                                for hh in range(2):
                                    for kv, src, srck in ((0, kT2, "g_kT2"), (1, vT2, "g_vT2")):
                                        PE(lambda e, bb_=bb_, sl=d * 4 + hh * 2 + kv, src=src, hh=hh, c=cs_(d): e.transpose(
                                            bb_[:, sl, :], src[:, hh, c], ident_bf[:]), [srck, "ident_bf"], [bbk])
                            for d in range(2):
                                kview = bb_[:, 4 * d:4 * d + 4:2, :]
                                vview = bb_[:, 4 * d + 1:4 * d + 4:2, :]
                                DVE(lambda e, d=d, vview=vview: e.tensor_tensor(
                                    R4[:, 2 * d:2 * d + 2, 0:128], vview, bt[:, 0, 2 * d:2 * d + 2].unsqueeze(2).broadcast_to([128, 2, 128]),
                                    ALU.mult), [bbk, kbt], [kR4])
                                DVE(lambda e, d=d, kview=kview: e.tensor_tensor(
                                    R4[:, 2 * d:2 * d + 2, 128:256], kview, bt[:, 1, 2 * d:2 * d + 2].unsqueeze(2).broadcast_to([128, 2, 128]),
                                    ALU.mult), [bbk, kbt, kR4], [kR4])
                                DVE(lambda e, d=d, kview=kview: e.tensor_tensor(
                                    kd4[:, 2 * d:2 * d + 2, :], kview, gs[:, 1, 2 * d:2 * d + 2].unsqueeze(2).broadcast_to([128, 2, 128]),
                                    ALU.mult), [bbk, kgs], [kkd])
                            yield
                            for half in range(2):
                                b_, bk_ = gb.next()
                                bv = b_[:].rearrange("p a b -> p (a b)").rearrange("p (a b) -> p a b", a=2)
                                for sl in range(2):
                                    s_ = half * 2 + sl
                                    PE(lambda e, bv=bv, sl=sl, s_=s_, TT_=TT_: e.matmul(bv[:, sl, :], TT_[:, s_, :], R4[:, s_, :],
                                                                                     start=True, stop=True), [TTk, kR4], [bk_])
                                if half == 0:
                                    ACT(lambda e, bv=bv: e.copy(UW4[:, 0:2, :], bv), [bk_], [kUW])
                                else:
                                    DVE(lambda e, bv=bv: e.tensor_copy(UW4[:, 2:4, :], bv), [bk_], [kUW])
                            yield
                            bw, bwk = gb.next()
                            for s_ in range(4):
                                PE(lambda e, bw=bw, s_=s_: e.matmul(bw[:, s_, :], UW4[:, s_, 128:256], kd4[:, s_, :], start=True, stop=True),
                                   [kUW, kkd], [bwk])
                            bq, bqk_ = gb.next()
                            for s_ in range(4):
                                PE(lambda e, bq=bq, s_=s_: e.matmul(bq[:, s_, :], UW4[:, s_, 128:256], qkT4[:, s_, :], start=True, stop=True),
                                   [kUW, kqkT], [bqk_])
                            ACT(lambda e, bw=bw: e.activation(Wn4[:], bw[:], AF.Copy, scale=-1.0), [bwk], [kWn])
                            for d in range(2):
                                DVE(lambda e, d=d, c=cs_(d): e.tensor_tensor(qt4[:, 2 * d:2 * d + 2, :], qT2[:, :, c], E4[:, 2 * d:2 * d + 2, :],
                                                                            ALU.mult), ["g_qT2", kE], [kqt])
                            DVE(lambda e, bq=bq: e.tensor_tensor(qt4[:], qt4[:], bq[:], ALU.subtract), [bqk_, kqt], [kqt])
                            yield
                            So, Sok = S4[n % 2], "g_S%d" % (n % 2)
                            Sn, Snk = S4[(n + 1) % 2], "g_S%d" % ((n + 1) % 2)
                            bo, bok = gb.next()
                            for s_ in range(4):
                                PE(lambda e, bo=bo, s_=s_: e.matmul(bo[:, s_, :], qkT4[:, s_, :], UW4[:, s_, 0:128], start=True, stop=False),
                                   [kqkT, kUW], [bok])
                                PE(lambda e, bo=bo, s_=s_, So=So: e.matmul(bo[:, s_, :], qt4[:, s_, :], So[:, s_, :], start=False, stop=True),
                                   [kqt, Sok], [bok])
                            bs, bsk = gb.next()
                            for s_ in range(4):
                                PE(lambda e, bs=bs, s_=s_: e.matmul(bs[:, s_, :], kd4[:, s_, :], UW4[:, s_, 0:128], start=True, stop=False),
                                   [kUW, kkd], [bsk])
                                PE(lambda e, bs=bs, s_=s_, So=So: e.matmul(bs[:, s_, :], Wn4[:, s_, :], So[:, s_, :], start=False, stop=True),
                                   [kWn, Sok], [bsk])
                            POOL(lambda e, So=So: e.tensor_tensor(Se4[:], So[:], gs[:, 2, :].unsqueeze(2).broadcast_to([128, 4, 128]), ALU.mult),
                                 [Sok, kgs, kE], [kSe])
                            DVE(lambda e, bs=bs, Sn=Sn: e.tensor_tensor(Sn[:], Se4[:], bs[:], ALU.add), [bsk, kSe], [Snk])
                            if n < 8:
                                ACT(lambda e, bo=bo, n=n: e.copy(ostore[:, n, :, :], bo[:]), [bok], ["g_ostore%d" % n])
                            else:
                                m_ = 15 - n
                                DVE(lambda e, bo=bo, m_=m_: e.tensor_tensor(os4[:, 0:2, :], bo[:, 0:2, :], ostore[:, m_, 2:4, :], ALU.add),
                                    [bok, "g_ostore%d" % m_], [kos])
                                DVE(lambda e, bo=bo, m_=m_: e.tensor_tensor(os4[:, 2:4, :], bo[:, 2:4, :], ostore[:, m_, 0:2, :], ALU.add),
                                    [bok, "g_ostore%d" % m_, kos], [kos])
                                ACT(lambda e: e.activation(sq4[:], os4[:], AF.Square), [kos], [ksq])
                                DVE(lambda e: e.reduce_sum(st4[:, 0, :], sq4[:], AX.X), [ksq], [kst])
                                ACT(lambda e: e.activation(st4[:, 1, :], st4[:, 0, :], AF.Ln, bias=epsc[:, 0:1], scale=1.0 / HD), [kst, "epsc"], [kst])
                                ACT(lambda e: e.activation(st4[:, 1, :], st4[:, 1, :], AF.Exp, scale=-0.5), [kst], [kst])
                                DVE(lambda e: e.tensor_tensor(os4[:], os4[:], st4[:, 1, :].unsqueeze(2).broadcast_to([128, 4, 128]), ALU.mult),
                                    [kos, kst], [kos])
                                DVE(lambda e: e.tensor_tensor(os4[:], os4[:], dnw[:].unsqueeze(1).broadcast_to([128, 4, 128]), ALU.mult),
                                    [kos, "dnw"], [kos])
                                for d in range(2):
                                    DVE(lambda e, d=d, c=chunk[d]: e.tensor_tensor(
                                        y4[:, 2 * d:2 * d + 2, :], os4[:, 2 * d:2 * d + 2, :],
                                        zs[:, c, :].rearrange("p (a b) -> p a b", a=2), ALU.mult), [kos, "g_zs"], [ky4])
                                def ytail(y4=y4, ky4=ky4, yT4=B["yT4"], kyT=K_("yT4"), cs0=cs_(0), cs1=cs_(1)):
                                    bb_, bbk = gbb.next()
                                    for s_ in range(4):
                                        PE(lambda e, bb_=bb_, s_=s_: e.transpose(bb_[:, s_, :], y4[:, s_, :], ident_bf[:]), [ky4, "ident_bf"], [bbk])
                                    ACT(lambda e, bb_=bb_: e.copy(yT4[:], bb_[:, 0:4, :]), [bbk], [kyT])
                                    for d, c in ((0, cs0), (1, cs1)):
                                        DSP(lambda e, d=d, c=c: e.dma_start(
                                            out=mix_d[s, 1024 + hp * 256: 1024 + (hp + 1) * 256, c].rearrange("(a p) l -> p a l", p=128),
                                            in_=yT4[:, 2 * d:2 * d + 2, :]), [kyT], ["mix_d"])
                                deferred.append(ytail)
                            yield

                        deferred = []
                        for n0 in range(0, NIT, NSETS):
                            gens = [iteration(n0 + i_, i_) for i_ in range(min(NSETS, NIT - n0))]
                            alive = [True] * len(gens)
                            stepi = 0
                            while any(alive):
                                for gi in range(len(gens)):
                                    if alive[gi]:
                                        try:
                                            next(gens[gi])
                                        except StopIteration:
                                            alive[gi] = False
                                stepi += 1
                                if stepi in (3, 5, 7, 9) and deferred:
                                    deferred.pop(0)()
                        while deferred:
                            deferred.pop(0)()
                        S.emit()

        for s in range(nslot):
            def gates_prefetch(sg):
                wba = sbt(sg, "wba", [128, KT, 32], BF16)
                DPL(lambda e: e.dma_start(out=wba[:], in_=w_in[:, OFF_B:OFF_B + 32].rearrange("(k p) c -> p k c", p=128)),
                    [], ["wba"])
                return wba

            def gates_phase(sg, wba):
                if True:
                    ba = sbt(sg, "ba", [128, TT, 32], F32)
                    tmpg = sbt(sg, "tmpg", [128, TT, 16], F32)
                    gp = pst(sg, "gp", [128, TT, 32], F32)
                    for t in range(TT):
                        for k in range(KT):
                            PE(lambda e, t=t, k=k: e.matmul(gp[:, t, :], R1[:, k, t * 128:(t + 1) * 128], wba[:, k, :],
                                                            start=(k == 0), stop=(k == KT - 1)),
                               ["R1_%d" % t, "wba"], ["gp"])
                    ACT(lambda e: e.copy(ba[:], gp[:]), ["gp"], ["ba"])
                    ACT(lambda e: e.activation(tmpg[:], ba[:, :, 0:16], AF.Exp, scale=-1.0), ["ba"], ["tmpg"])
                    DVE(lambda e: e.tensor_scalar(tmpg[:], tmpg[:], 1.0, None, ALU.add), ["tmpg"], ["tmpg"])
                    DVE(lambda e: e.reciprocal(beta_all[:], tmpg[:]), ["tmpg"], ["beta_all"])
                    DVE(lambda e: e.tensor_tensor(tmpg[:], ba[:, :, 16:32],
                                                  dtb[:].unsqueeze(1).broadcast_to([128, TT, 16]), ALU.add),
                        ["ba", "dtb", "beta_all"], ["tmpg"])
                    ACT(lambda e: e.activation(tmpg[:], tmpg[:], AF.Exp), ["tmpg"], ["tmpg"])
                    ACT(lambda e: e.activation(tmpg[:], tmpg[:], AF.Ln, bias=1.0), ["tmpg"], ["tmpg"])
                    DVE(lambda e: e.tensor_tensor(g_all[:], tmpg[:], negA[:].unsqueeze(1).broadcast_to([128, TT, 16]),
                                                  ALU.mult), ["tmpg", "negA"], ["g_all"])

            sR = contextlib.ExitStack()
            R1 = sbt(sR, "R1a", [128, KT, L], BF16)
            if "A" in phases:
                with contextlib.ExitStack() as sa:
                    wba_ = gates_prefetch(sa) if "G" in phases else None
                    norm_transpose(sa, lambda t: x[s, t * 128:(t + 1) * 128, :], nw_mix, False)
                    for k in range(KT):
                        DSP(lambda e, k=k: e.dma_start(out=xn_s[k], in_=R1[:, k, :]), R1all, ["xn_s%d" % k])
                    if "G" in phases:
                        gates_phase(sa, wba_)
                    S.emit()

            if "N" in phases:
                with contextlib.ExitStack() as sn:
                    wq = Ring([sbt(sn, "wq%d" % i, [128, KT, 128], BF16) for i in range(2)], "wq")
                    wk = Ring([sbt(sn, "wk%d" % i, [128, KT, 128], BF16) for i in range(2)], "wk")
                    wv = Ring([sbt(sn, "wv%d" % i, [128, KT, 128], BF16) for i in range(2)], "wv")
                    qTs = [sbt(sn, "na_qT%d" % i, [128, L], BF16) for i in range(2)]
                    kTs = [sbt(sn, "na_kT%d" % i, [128, L], BF16) for i in range(2)]
                    vvs = [sbt(sn, "na_v%d" % i, [128, TT, 128], BF16) for i in range(2)]
                    Bms = [sbt(sn, "na_Bm%d" % i, [128, 5, 640], F32) for i in range(2)]
                    Sb = Ring([sbt(sn, "na_Sb%d" % i, [128, 640], F32) for i in range(2)], "na_Sb")
                    Pb = Ring([sbt(sn, "na_P%d" % i, [128, 640], BF16) for i in range(2)], "na_P")
                    Pn = Ring([sbt(sn, "na_Pn%d" % i, [128, 640], BF16) for i in range(2)], "na_Pn")
                    PTs = Ring([sbt(sn, "na_PT%d" % i, [128, 5, 128], BF16) for i in range(2)], "na_PT")
                    st8 = Ring([sbt(sn, "na_st%d" % i, [128, 4], F32) for i in range(2)], "na_st")
                    ona = Ring([sbt(sn, "na_o%d" % i, [128, L], BF16) for i in range(2)], "na_o")
                    psA = Ring([pst(sn, "na_psA%d" % i, [128, 1024], F32) for i in range(2)], "na_psA")
                    ptp = Ring([pst(sn, "na_ptp%d" % i, [128, 8, 128], BF16) for i in range(2)], "na_ptp")
                    pin = Ring([pst(sn, "na_pin%d" % i, [128, 512], F32) for i in range(2)], "na_pin")
                    for bi in range(2):
                        POOL(lambda e, bi=bi: e.memset(Bms[bi][:], NEG), [], ["Bm%d_%d_%d" % (bi, a_, b_) for a_ in range(5) for b_ in range(2)])
                    cls_tab = [((0, 7), (0, 6)), ((0, 5), (0, 4)), ((0, 3), (1, 3)), ((2, 3), (2, 2)), ((2, 1), (2, 0))]

                    def inproj(h):
                        bi = h % 2
                        qT, kT, vv, Bm = qTs[bi], kTs[bi], vvs[bi], Bms[bi]
                        wqt, wqk = wq.next()
                        wkt, wkk = wk.next()
                        wvt, wvk = wv.next()
                        for (wt, wkey, off) in ((wqt, wqk, 0), (wkt, wkk, 1024), (wvt, wvk, 2048)):
                            DPL(lambda e, wt=wt, off=off, h=h: e.dma_start(
                                out=wt[:], in_=w_in[:, off + h * 128: off + (h + 1) * 128].rearrange("(k p) c -> p k c", p=128)),
                                [], [wkey])
                        for cl in range(5):
                            for e_ in range(2):
                                j0, dr0 = cls_tab[cl][e_]
                                DSP(lambda e, cl=cl, e_=e_, j0=j0, dr0=dr0, h=h, Bm=Bm: e.dma_start(
                                    out=Bm[e_ * 64:(e_ + 1) * 64, cl, j0 * 64:(j0 + 8) * 64],
                                    in_=tb_d[:, h, dr0:dr0 + 8, :].rearrange("p a b -> p (a b)")),
                                    ["tb%d" % h], ["Bm%d_%d_%d" % (bi, cl, e_)])
                        yield
                        for tg in range(4):
                            pq, pqk = pin.next()
                            for k in range(KT):
                                PE(lambda e, pq=pq, k=k, tg=tg, wqt=wqt: e.matmul(
                                    pq[:], wqt[:, k, :], R1[:, k, tg * 512:(tg + 1) * 512], start=(k == 0), stop=(k == KT - 1)),
                                   R1all + [wqk], [pqk])
                            ACT(lambda e, pq=pq, tg=tg, qT=qT: e.activation(qT[:, tg * 512:(tg + 1) * 512], pq[:], AF.Copy,
                                                                      scale=float(HD ** -0.5)), [pqk], ["na_qT%d" % bi])
                            yield
                            pk, pkk = pin.next()
                            for k in range(KT):
                                PE(lambda e, pk=pk, k=k, tg=tg, wkt=wkt: e.matmul(
                                    pk[:], wkt[:, k, :], R1[:, k, tg * 512:(tg + 1) * 512], start=(k == 0), stop=(k == KT - 1)),
                                   R1all + [wkk], [pkk])
                            DVE(lambda e, pk=pk, tg=tg, kT=kT: e.tensor_copy(kT[:, tg * 512:(tg + 1) * 512], pk[:]), [pkk], ["na_kT%d" % bi])
                            yield
                        for t4 in range(4):
                            pv, pvk = pin.next()
                            for tl in range(4):
                                t = t4 * 4 + tl
                                for k in range(KT):
                                    PE(lambda e, pv=pv, k=k, t=t, tl=tl, wvt=wvt: e.matmul(
                                        pv[:, tl * 128:(tl + 1) * 128], R1[:, k, t * 128:(t + 1) * 128], wvt[:, k, :],
                                        start=(k == 0), stop=(k == KT - 1)), R1all + [wvk], [pvk])
                                if tl == 1:
                                    yield
                            ACT(lambda e, pv=pv, t4=t4, vv=vv: e.copy(vv[:, t4 * 4:(t4 + 1) * 4, :].rearrange("p a b -> p (a b)"), pv[:]),
                                [pvk], ["na_v%d" % bi])
                            yield

                    def unit(h, p, ot, otk):
                        bi = h % 2
                        qT, kT, vv, Bm = qTs[bi], kTs[bi], vvs[bi], Bms[bi]
                        kq, kk_, kv_ = "na_qT%d" % bi, "na_kT%d" % bi, "na_v%d" % bi
                        ts = min(max(p - 2, 0), 11)
                        cl = 0 if p == 0 else 1 if p == 1 else 3 if p == 14 else 4 if p == 15 else 2
                        kbm = ["Bm%d_%d_0" % (bi, cl), "Bm%d_%d_1" % (bi, cl)]
                        pa, pak = psA.next()
                        sbf, sbk = Sb.next()
                        pbt, pbtk = Pb.next()
                        pnt, pnk = Pn.next()
                        ptt, pttk = PTs.next()
                        stt, sttk = st8.next()
                        pp, ppk = ptp.next()
                        PE(lambda e: e.matmul(pa[:, 0:512], qT[:, p * 128:(p + 1) * 128], kT[:, ts * 128: ts * 128 + 512], start=True, stop=True),
                           [kq, kk_], [pak])
                        PE(lambda e: e.matmul(pa[:, 512:640], qT[:, p * 128:(p + 1) * 128], kT[:, ts * 128 + 512: ts * 128 + 640],
                                              start=True, stop=True), [kq, kk_], [pak])
                        yield
                        DVE(lambda e: e.tensor_tensor(sbf[:], pa[:, 0:640], Bm[:, cl, :], ALU.add), [pak] + kbm, [sbk])
                        DVE(lambda e: e.reduce_max(stt[:, 1:2], sbf[:], AX.X, negate=True), [sbk], [sttk])
                        yield
                        ACT(lambda e: e.activation(pbt[:], sbf[:], AF.Exp, bias=stt[:, 1:2], accum_out=stt[:, 2:3]), [sbk, sttk], [pbtk, sttk])
                        yield
                        DVE(lambda e: e.reciprocal(stt[:, 3:4], stt[:, 2:3]), [sttk], [sttk])
                        DVE(lambda e: e.tensor_scalar(pnt[:], pbt[:], stt[:, 3:4], None, ALU.mult), [pbtk, sttk], [pnk])
                        yield
                        for j in range(5):
                            PE(lambda e, j=j: e.transpose(pp[:, j, :], pnt[:, j * 128:(j + 1) * 128], ident_bf[:]), [pnk, "ident_bf"], [ppk])
                        yield
                        ACT(lambda e: e.copy(ptt[:], pp[:, 0:5, :]), [ppk], [pttk])
                        yield
                        for j in range(5):
                            PE(lambda e, j=j: e.matmul(pa[:, 640:768], vv[:, ts + j, :], ptt[:, j, :], start=(j == 0), stop=(j == 4)),
                               [kv_, pttk], [pak])
                        yield
                        ACT(lambda e: e.copy(ot[:, p * 128:(p + 1) * 128], pa[:, 640:768]), [pak], [otk])
                        yield

                    def drive(gens):
                        alive = [True] * len(gens)
                        while any(alive):
                            for gi in range(len(gens)):
                                if alive[gi]:
                                    try:
                                        next(gens[gi])
                                    except StopIteration:
                                        alive[gi] = False

                    def prepass():
                        stg = Ring([sbt(sn, "stg%d" % i, [128, 4096], BF16) for i in range(3)], "stg")
                        for (wsrc, wdst, nm) in ((w_gate, wg_s, "wgs"), (w_up, wu_s, "wus")):
                            for k in range(KT):
                                for pc, (c0, wdt_) in enumerate(((0, 4096), (4096, DFF - 4096))):
                                    sg_, sgk_ = stg.next()
                                    DPL(lambda e, sg_=sg_, wsrc=wsrc, k=k, c0=c0, wdt_=wdt_: e.dma_start(
                                        out=sg_[:, 0:wdt_], in_=wsrc[k * 128:(k + 1) * 128, c0:c0 + wdt_]), [], [sgk_])
                                    DSP(lambda e, sg_=sg_, wdst=wdst, k=k, c0=c0, wdt_=wdt_: e.dma_start(
                                        out=wdst[c0 // 256:(c0 + wdt_) // 256, :, k, :].rearrange("f p c -> p f c"),
                                        in_=sg_[:, 0:wdt_].rearrange("p (f c) -> p f c", c=256)), [sgk_], ["%s_%d_%d" % (nm, k, pc)])
                                    yield
                        for r in range(FC):
                            sg_, sgk_ = stg.next()
                            DPL(lambda e, sg_=sg_, r=r: e.dma_start(out=sg_[:, 0:D], in_=w_down[r * 128:(r + 1) * 128, :]), [], [sgk_])
                            DSP(lambda e, sg_=sg_, r=r: e.dma_start(
                                out=wd_s[r // 4, :, :, r % 4, :].rearrange("g p c -> p g c"),
                                in_=sg_[:, 0:D].rearrange("p (g c) -> p g c", c=512)), [sgk_], ["wds_%d" % r])
                            yield

                    pre = prepass() if s == 0 else iter(())
                    drive([inproj(0)])
                    for h in range(8):
                        ot, otk = ona.next()
                        filler = inproj(h + 1) if h + 1 < 8 else iter(())
                        for p0 in range(0, 16, 2):
                            gens = [unit(h, p0, ot, otk), unit(h, p0 + 1, ot, otk)]
                            alive = [True, True]
                            stepi = 0
                            while any(alive):
                                for gi in range(2):
                                    if alive[gi]:
                                        try:
                                            next(gens[gi])
                                        except StopIteration:
                                            alive[gi] = False
                                stepi += 1
                                if stepi in (4, 6):
                                    next(filler, None)
                                if stepi % 4 == 0:
                                    next(pre, None)
                        for _ in filler:
                            pass
                        DSP(lambda e, ot=ot, h=h: e.dma_start(out=mix_d[s, h * 128:(h + 1) * 128, :], in_=ot[:]), [otk], ["mix_d"])
                    for _ in pre:
                        pass
                    S.emit()


            sR.close()
            if "D" in phases:
                gdn_phase(s)

            if "O" in phases:
                with contextlib.ExitStack() as so:
                    mixT = sbt(so, "mixT", [128, KT, L], BF16)
                    wo = Ring([sbt(so, "wo%d" % i, [128, KT, 512], BF16) for i in range(2)], "wo")
                    xc = Ring([sbt(so, "xc%d" % i, [128, 512], F32) for i in range(3)], "xc")
                    hc = Ring([sbt(so, "hc%d" % i, [128, 512], F32) for i in range(3)], "hc")
                    junk = sbt(so, "o_junk", [128, 512], BF16)
                    po_ = Ring([pst(so, "po%d" % i, [128, 512], F32) for i in range(4)], "po")
                    for k in range(KT):
                        (DSP if k % 2 == 0 else DPL)(lambda e, k=k: e.dma_start(out=mixT[:, k, :], in_=mix_d[s, k * 128:(k + 1) * 128, :]),
                            ["mix_d"], ["mixT%d" % k])
                    for cg in range(4):
                        wot, wok = wo.next()
                        DPL(lambda e, wot=wot, cg=cg: e.dma_start(
                            out=wot[:], in_=w_out[:, cg * 512:(cg + 1) * 512].rearrange("(k p) c -> p k c", p=128)), [], [wok])
                        for t in range(TT):
                            xct, xck = xc.next()
                            hct, hck = hc.next()
                            pt, ptk = po_.next()
                            DSP(lambda e, xct=xct, t=t, cg=cg: e.dma_start(
                                out=xct[:], in_=x[s, t * 128:(t + 1) * 128, cg * 512:(cg + 1) * 512]), [], [xck])
                            for k in range(KT):
                                PE(lambda e, pt=pt, k=k, t=t, wot=wot: e.matmul(
                                    pt[:], mixT[:, k, t * 128:(t + 1) * 128], wot[:, k, :], start=(k == 0), stop=(k == KT - 1)),
                                   ["mixT%d" % k, wok], [ptk])
                            DVE(lambda e, hct=hct, pt=pt, xct=xct: e.tensor_tensor(hct[:], pt[:], xct[:], ALU.add),
                                [ptk, xck], [hck])
                            ACT(lambda e, hct=hct, t=t, cg=cg: e.activation(junk[:], hct[:], AF.Square,
                                                                            accum_out=ssp[:, t, cg:cg + 1]),
                                [hck], ["o_junk", "ssp"])
                            DSP(lambda e, hct=hct, t=t, cg=cg: e.dma_start(
                                out=h_d[s, t * 128:(t + 1) * 128, cg * 512:(cg + 1) * 512], in_=hct[:]),
                                [hck], ["srcrows%d" % t])
                    DVE(lambda e: e.reduce_sum(ss_all[:], ssp[:], AX.X), ["ssp"], ["ss_all"])
                    ACT(lambda e: e.activation(rstd_all[:], ss_all[:], AF.Ln, bias=epsc[:, 0:1], scale=1.0 / D), ["ss_all", "epsc"], ["rstd_all0"])
                    ACT(lambda e: e.activation(rstd_all[:], rstd_all[:], AF.Exp, scale=-0.5), ["rstd_all0"],
                        ["rstd%d" % t for t in range(TT)])
                    S.emit()
            sR = contextlib.ExitStack()
            R1 = sbt(sR, "R1b", [128, KT, L], BF16)
            if "O" in phases:
                with contextlib.ExitStack() as so2:
                    norm_transpose(so2, lambda t: h_d[s, t * 128:(t + 1) * 128, :], nw_ffn, True)
                    S.emit()

            if "F" in phases:
                with contextlib.ExitStack() as sf:
                    actb = sbt(sf, "actb", [128, FC, 512], BF16)
                    wg = Ring([sbt(sf, "wg%d" % i, [128, KT, 256], BF16) for i in range(2)], "wg")
                    wu = Ring([sbt(sf, "wu%d" % i, [128, KT, 256], BF16) for i in range(2)], "wu")
                    wd = Ring([sbt(sf, "wd%d" % i, [128, 4, 512], BF16) for i in range(2)], "wd")
                    sgt = Ring([sbt(sf, "sg%d" % i, [128, 512], F32) for i in range(2)], "sg")
                    hcr = Ring([sbt(sf, "fh%d" % i, [128, 512], F32) for i in range(4)], "fh")
                    ycr = Ring([sbt(sf, "fy%d" % i, [128, 512], F32) for i in range(3)], "fy")
                    junk = sbt(sf, "f_junk", [128, 512], BF16)
                    pg = Ring([pst(sf, "pg%d" % i, [128, 512], F32) for i in range(2)], "pg")
                    pu = Ring([pst(sf, "pu%d" % i, [128, 512], F32) for i in range(2)], "pu")
                    pd = [pst(sf, "pd%d" % i, [128, 512], F32) for i in range(4)]
                    for blk in range(4):
                        c0 = blk * 512
                        for f2 in range(FC // 2):
                            wgt, wgk = wg.next()
                            wut, wuk = wu.next()
                            pc_ = 0 if f2 < 16 else 1
                            DSP(lambda e, wgt=wgt, f2=f2: e.dma_start(out=wgt[:], in_=wg_s[f2]),
                                ["wgs_%d_%d" % (k, pc_) for k in range(KT)], [wgk])
                            DSP(lambda e, wut=wut, f2=f2: e.dma_start(out=wut[:], in_=wu_s[f2]),
                                ["wus_%d_%d" % (k, pc_) for k in range(KT)], [wuk])
                            for fl in range(2):
                                fc = f2 * 2 + fl
                                pgt, pgk = pg.next()
                                put, puk = pu.next()
                                sg, sgk = sgt.next()
                                for k in range(KT):
                                    PE(lambda e, pgt=pgt, k=k, fl=fl, wgt=wgt, c0=c0: e.matmul(
                                        pgt[:], wgt[:, k, fl * 128:(fl + 1) * 128], R1[:, k, c0:c0 + 512],
                                        start=(k == 0), stop=(k == KT - 1)), R1all + [wgk], [pgk])
                                for k in range(KT):
                                    PE(lambda e, put=put, k=k, fl=fl, wut=wut, c0=c0: e.matmul(
                                        put[:], wut[:, k, fl * 128:(fl + 1) * 128], R1[:, k, c0:c0 + 512],
                                        start=(k == 0), stop=(k == KT - 1)), R1all + [wuk], [puk])
                                ACT(lambda e, sg=sg, pgt=pgt: e.activation(sg[:], pgt[:], AF.Silu), [pgk], [sgk])
                                DVE(lambda e, sg=sg, put=put, fc=fc: e.tensor_tensor(actb[:, fc, :], sg[:], put[:], ALU.mult),
                                    [sgk, puk], ["actb"])
                        for cg in range(4):
                            hcts = []
                            for tt in range(4):
                                t = blk * 4 + tt
                                hct, hck = hcr.next()
                                DPL(lambda e, hct=hct, t=t, cg=cg: e.dma_start(
                                    out=hct[:], in_=h_d[s, t * 128:(t + 1) * 128, cg * 512:(cg + 1) * 512]),
                                    ["srcrows%d" % t], [hck])
                                hcts.append((hct, hck))
                            for f4 in range(FC // 4):
                                wdt, wdk = wd.next()
                                DSP(lambda e, wdt=wdt, f4=f4, cg=cg: e.dma_start(out=wdt[:], in_=wd_s[f4, cg]),
                                    ["wds_%d" % (f4 * 4 + a_) for a_ in range(4)], [wdk])
                                for fl in range(4):
                                    fc = f4 * 4 + fl
                                    for tt in range(4):
                                        PE(lambda e, tt=tt, fc=fc, fl=fl, wdt=wdt: e.matmul(
                                            pd[tt][:], actb[:, fc, tt * 128:(tt + 1) * 128], wdt[:, fl, :],
                                            start=(fc == 0), stop=(fc == FC - 1)), ["actb", wdk], ["pd%d" % tt])
                            for tt in range(4):
                                t = blk * 4 + tt
                                hct, hck = hcts[tt]
                                yct, yck = ycr.next()
                                DVE(lambda e, yct=yct, tt=tt, hct=hct: e.tensor_tensor(yct[:], pd[tt][:], hct[:], ALU.add),
                                    ["pd%d" % tt, hck], [yck])
                                ACT(lambda e, yct=yct, t=t, cg=cg: e.activation(junk[:], yct[:], AF.Square,
                                                                                accum_out=ssp[:, t, cg:cg + 1]),
                                    [yck], ["f_junk", "ssp"])
                                DPL(lambda e, yct=yct, t=t, cg=cg: e.dma_start(
                                    out=y[s, t * 128:(t + 1) * 128, cg * 512:(cg + 1) * 512], in_=yct[:]), [yck], ["yrows%d" % t])
                    DVE(lambda e: e.reduce_sum(ss_all[:], ssp[:], AX.X), ["ssp"], ["ss_all"])
                    ACT(lambda e: e.activation(rstd_all[:], ss_all[:], AF.Ln, bias=epsc[:, 0:1], scale=1.0 / D), ["ss_all", "epsc"], ["rstd_all0"])
                    ACT(lambda e: e.activation(rstd_all[:], rstd_all[:], AF.Exp, scale=-0.5), ["rstd_all0"], ["rstd_allF"])
                    S.emit()
                with contextlib.ExitStack() as sf2:
                    fw = sbt(sf2, "fw", [128, D], F32)
                    yt = Ring([sbt(sf2, "yt%d" % i, [128, D], F32) for i in range(4)], "yt")
                    DSP(lambda e: e.dma_start(out=fw[:], in_=nw_fin_d.partition_broadcast(128)), [], ["fw"])
                    for t in range(TT):
                        ytt, ytk = yt.next()
                        DSP(lambda e, ytt=ytt, t=t: e.dma_start(out=ytt[:], in_=y[s, t * 128:(t + 1) * 128, :]),
                            ["yrows%d" % t], [ytk])
                        DVE(lambda e, ytt=ytt, t=t: e.scalar_tensor_tensor(ytt[:], ytt[:], rstd_all[:, t:t + 1], fw[:],
                                                                           ALU.mult, ALU.mult),
                            [ytk, "rstd_allF", "fw"], [ytk])
                        DPL(lambda e, ytt=ytt, t=t: e.dma_start(out=y[s, t * 128:(t + 1) * 128, :], in_=ytt[:]),
                            [ytk], ["yrows%d" % t])
                    S.emit()
            sR.close()
        S.emit(final=True)
    return nc


def host_consts(inp):
    f32 = np.float32
    c = {}
    c["consts"] = make_consts()
    c["nw_mix"] = np.ascontiguousarray(np.asarray(inp["norm_mix_w"], f32)[0].reshape(KT, 128).T)
    c["nw_ffn"] = np.ascontiguousarray(np.asarray(inp["norm_ffn_w"], f32)[0].reshape(KT, 128).T)
    c["nw_fin"] = np.ascontiguousarray(np.asarray(inp["final_norm_w"], f32).reshape(1, D))
    c["rpb"] = np.ascontiguousarray(np.asarray(inp["na_rpb"], f32)[0].reshape(1, -1))
    cw = np.asarray(inp["dn_conv_w"], f32)[0]
    c["convw"] = np.ascontiguousarray(cw.T.reshape(24, 128, 5).transpose(1, 0, 2).reshape(128, 120))
    c["alog"] = np.ascontiguousarray(np.asarray(inp["dn_A_log"], f32)[0].reshape(1, 16))
    c["dtb"] = np.ascontiguousarray(np.asarray(inp["dn_dt_bias"], f32)[0].reshape(1, 16))
    c["dnw"] = np.ascontiguousarray(np.asarray(inp["dn_norm_w"], f32)[0].reshape(1, 128))
    c["w_in"] = np.ascontiguousarray(np.asarray(inp["w_in"], f32)[0])
    c["w_out"] = np.ascontiguousarray(np.asarray(inp["w_out"], f32)[0])
    c["w_gate"] = np.ascontiguousarray(np.asarray(inp["w_gate"], f32)[0])
    c["w_up"] = np.ascontiguousarray(np.asarray(inp["w_up"], f32)[0])
    c["w_down"] = np.ascontiguousarray(np.asarray(inp["w_down"], f32)[0])
    return c


def kernel(**inputs):
    xp = np.asarray(inputs["x_prompt"], np.float32)
    xs = np.asarray(inputs["x_sample"], np.float32)
    shared = host_consts(inputs)
    nc = build(nslot=2)
    in_maps = []
    for c in range(8):
        m = dict(shared)
        m["x"] = np.ascontiguousarray(np.stack([xp[c], xs[c % 4]], axis=0))
        in_maps.append(m)
    res = run_bass_kernel_spmd(nc, in_maps, core_ids=list(range(8)))
    yp = np.stack([np.asarray(res.results[c]["y"])[0] for c in range(8)], axis=0)
    ys = np.stack([np.asarray(res.results[c]["y"])[1] for c in range(4)], axis=0)
    return (yp.astype(np.float32), ys.astype(np.float32))
```
